# Optimizing a Trainium2 kernel written in Bass

```python
import jax, jax.numpy as jnp
from jax import lax
import numpy as np

D_MODEL = 1024
BATCH = 2
SEQ = 8192
DEPTH = 2

CHUNK = 64
POOL_WIDTH = D_MODEL // 2
POOL_WINDOWS = (2, 4, 8, 16)
POOL_GROUPS = len(POOL_WINDOWS)
POOL_GROUP_DIM = POOL_WIDTH // POOL_GROUPS
HGRN_HEAD_DIM = 128
HGRN_HEADS = (D_MODEL // 2) // HGRN_HEAD_DIM
HGRN_WIDTH = HGRN_HEADS * HGRN_HEAD_DIM
N_BRANCHES = 2
IN_COLS = POOL_WIDTH + 4 * HGRN_WIDTH + N_BRANCHES * D_MODEL
D_FF = 128 * ((8 * D_MODEL // 3 + 127) // 128)
CONV_WIDTH = 3
EPS = 1e-6

kernel_name = "hybrid_pool_hgrn2_convglu_trunk"


def rmsnorm(x, g):
    xf = x.astype(jnp.float32)
    y = xf * lax.rsqrt(jnp.mean(xf * xf, axis=-1, keepdims=True) + EPS) * g.astype(jnp.float32)
    return y.astype(x.dtype)


def pool_mixer(u, pool_w, pool_scale):
    B, S, _ = u.shape
    ug = u.reshape(B, S, POOL_GROUPS, POOL_GROUP_DIM).astype(jnp.float32)
    cs = jnp.cumsum(ug, axis=1)
    t = jnp.arange(1, S + 1, dtype=jnp.float32)
    outs = []
    for g, w in enumerate(POOL_WINDOWS):
        c_pad = jnp.pad(cs[:, :, g], ((0, 0), (w, 0), (0, 0)))
        win_sum = c_pad[:, w:] - c_pad[:, :S]
        count = jnp.minimum(t, float(w))[None, :, None]
        outs.append(win_sum / count - ug[:, :, g])
    pooled = jnp.stack(outs, axis=2).astype(u.dtype)
    mixed = jnp.einsum('bsgc,gcd->bsgd', pooled, pool_w).reshape(B, S, POOL_WIDTH)
    return mixed * pool_scale


def hgrn2_mixer(zq, zf, zi, zo, lb, norm_g):
    B, S, _ = zq.shape
    n_chunks = S // CHUNK
    H, Dh = HGRN_HEADS, HGRN_HEAD_DIM

    def heads(a):
        return a.reshape(B, S, H, Dh).astype(jnp.float32)

    q = jax.nn.silu(heads(zq))
    lbh = lb.reshape(H, Dh).astype(jnp.float32)
    f = lbh + (1.0 - lbh) * jax.nn.sigmoid(heads(zf))
    log_f = jnp.log(f)
    k = 1.0 - f
    v = heads(zi)

    def to_chunks(a):
        return a.reshape(B, n_chunks, CHUNK, H, Dh).transpose(1, 0, 3, 2, 4)

    causal = jnp.tril(jnp.ones((CHUNK, CHUNK), dtype=bool))

    def step(state, xs):
        qc, kc, vc, gc = xs
        b = jnp.cumsum(gc, axis=2)
        diff = b[:, :, :, None, :] - b[:, :, None, :, :]
        decay = jnp.exp(jnp.where(causal[:, :, None], diff, -jnp.inf))
        attn = jnp.einsum('bhtk,bhtsk,bhsk->bhts', qc, decay, kc)
        o = (jnp.einsum('bhts,bhsv->bhtv', attn, vc)
             + jnp.einsum('bhtk,bhkv->bhtv', qc * jnp.exp(b), state))
        b_last = b[:, :, -1:, :]
        new_state = (jnp.exp(b_last[:, :, 0, :])[..., None] * state
                     + jnp.einsum('bhsk,bhsv->bhkv', kc * jnp.exp(b_last - b), vc))
        return new_state, o

    s0 = jnp.zeros((B, H, Dh, Dh), jnp.float32)
    _, o = lax.scan(step, s0, (to_chunks(q), to_chunks(k), to_chunks(v), to_chunks(log_f)))
    o = o.transpose(1, 0, 3, 2, 4).reshape(B, S, H, Dh)
    o = o * lax.rsqrt(jnp.mean(o * o, axis=-1, keepdims=True) + EPS) * norm_g.astype(jnp.float32)
    o = o * jax.nn.silu(heads(zo))
    return o.reshape(B, S, HGRN_WIDTH).astype(zq.dtype)


def conv_glu_ffn(x, w_up, conv_w, conv_b, w_down):
    S = x.shape[1]
    h = x @ w_up
    hp = jnp.pad(h, ((0, 0), (CONV_WIDTH - 1, 0), (0, 0)))
    hc = sum(conv_w[j] * hp[:, j:j + S] for j in range(CONV_WIDTH)) + conv_b
    val, gate = jnp.split(hc, 2, axis=-1)
    return (jax.nn.silu(gate) * val) @ w_down


def setup_inputs(seed: int = 0) -> dict:
    key = jax.random.key(seed)
    ks = jax.random.split(key, 20)
    f32 = jnp.float32
    nrm = lambda k, shape, scale: (jax.random.normal(k, shape, f32) * scale).astype(f32)
    L = DEPTH
    return {
        "x": nrm(ks[0], (BATCH, SEQ, D_MODEL), 1.0),
        "norm1_g": 1.0 + nrm(ks[1], (L, D_MODEL), 0.02),
        "w_in": nrm(ks[2], (L, D_MODEL, IN_COLS), D_MODEL ** -0.5),
        "b_gate": nrm(ks[3], (L, N_BRANCHES * D_MODEL), 0.01),
        "pool_w": nrm(ks[4], (L, POOL_GROUPS, POOL_GROUP_DIM, POOL_GROUP_DIM), POOL_GROUP_DIM ** -0.5),
        "pool_scale": 1.0 + nrm(ks[5], (L, POOL_WIDTH), 0.02),
        "lb_logits": nrm(ks[6], (L, HGRN_WIDTH), 1.0),
        "hgrn_norm_g": 1.0 + nrm(ks[7], (L, HGRN_HEAD_DIM), 0.02),
        "w_pa": nrm(ks[8], (L, POOL_WIDTH, D_MODEL), POOL_WIDTH ** -0.5),
        "w_pb": nrm(ks[9], (L, HGRN_WIDTH, D_MODEL), HGRN_WIDTH ** -0.5),
        "w_o": nrm(ks[10], (L, D_MODEL, D_MODEL), D_MODEL ** -0.5),
        "norm2_g": 1.0 + nrm(ks[11], (L, D_MODEL), 0.02),
        "w_up": nrm(ks[12], (L, D_MODEL, 2 * D_FF), D_MODEL ** -0.5),
        "conv_w": nrm(ks[13], (L, CONV_WIDTH, 2 * D_FF), CONV_WIDTH ** -0.5),
        "conv_b": nrm(ks[14], (L, 2 * D_FF), 0.01),
        "w_down": nrm(ks[15], (L, D_FF, D_MODEL), D_FF ** -0.5),
        "final_g": 1.0 + nrm(ks[16], (D_MODEL,), 0.02),
    }


def reference(x, norm1_g, w_in, b_gate, pool_w, pool_scale, lb_logits, hgrn_norm_g,
              w_pa, w_pb, w_o, norm2_g, w_up, conv_w, conv_b, w_down, final_g):
    B, S, _ = x.shape
    lb_soft = jax.nn.softmax(lb_logits.astype(jnp.float32), axis=0)
    lb_cum = jnp.cumsum(lb_soft, axis=0)
    lower_bounds = lb_cum - lb_cum[0:1]

    splits = np.cumsum([POOL_WIDTH, HGRN_WIDTH, HGRN_WIDTH, HGRN_WIDTH, HGRN_WIDTH]).tolist()
    for l in range(DEPTH):
        xn = rmsnorm(x, norm1_g[l])
        z = xn @ w_in[l]
        u_pool, zq, zf, zi, zo, zg = jnp.split(z, splits, axis=-1)
        gates = jax.nn.sigmoid(zg + b_gate[l]).reshape(B, S, N_BRANCHES, D_MODEL)
        ya = pool_mixer(u_pool, pool_w[l], pool_scale[l]) @ w_pa[l]
        yb = hgrn2_mixer(zq, zf, zi, zo, lower_bounds[l], hgrn_norm_g[l]) @ w_pb[l]
        merged = gates[:, :, 0] * ya + gates[:, :, 1] * yb
        x = x + merged @ w_o[l]
        x = x + conv_glu_ffn(rmsnorm(x, norm2_g[l]), w_up[l], conv_w[l], conv_b[l], w_down[l])
    return rmsnorm(x, final_g)
```

```python
import os
import numpy as np
from contextlib import ExitStack
import concourse.bass as bass
import concourse.mybir as mybir
from concourse.bass_utils import run_bass_kernel_spmd

F32 = mybir.dt.float32
BF16 = mybir.dt.bfloat16
AF = mybir.ActivationFunctionType
ALU = mybir.AluOpType

D = 1024
SEQ = 8192
NB = 2
L = 2
NCORE = 8
OWN = 2048
HALO = 128
EXT = OWN + HALO
KC = 8
DFF = 2816
NFF = 22
INC = 4608
EPS = 1e-6
TMAX = 448
NPP = 232
NCST = 1240
NSLOT = 4
REUSE_BF16 = False
EXACT_L1 = True
SLOTB = 4096


class Buf:
    __slots__ = ("name", "w", "r")

    def __init__(self, name):
        self.name = name
        self.w = None
        self.r = []


class Eng:
    def __init__(self, name, sem):
        self.name = name
        self.sem = sem
        self.n = 0
        self.ops = []
        self.waited = {}


class Prog:
    def __init__(self, nc, stack):
        self.nc = nc
        self.stack = stack
        self.sems = {}
        self.eng = {}
        for name in ("tensor", "vector", "scalar", "gpsimd", "sync"):
            s = stack.enter_context(nc.semaphore("s_" + name))
            self.sems["e_" + name] = s
            self.eng[name] = Eng(name, "e_" + name)
        self.dma_cnt = {}
        self.nbuf = 0
        self.label = ""
        self.pe_labels = []

    def buf(self, name=None):
        self.nbuf += 1
        return Buf(name or ("b%d" % self.nbuf))

    def bufs(self, n, name="b"):
        return [self.buf("%s%d" % (name, i)) for i in range(n)]

    def dma_sem(self, name):
        key = "d_" + name
        if key not in self.sems:
            self.sems[key] = self.stack.enter_context(self.nc.semaphore(key))
            self.dma_cnt[key] = 0
        return key

    def _need(self, eng, tok, same_engine_ok=False):
        if tok is None:
            return
        key, val = tok
        if same_engine_ok and key == eng.sem:
            return
        if eng.waited.get(key, 0) >= val:
            return
        eng.waited[key] = val
        sem = self.sems[key]
        eng.ops.append(lambda e, sem=sem, val=val: e.wait_ge(sem, val))

    def _deps(self, eng, reads, writes):
        pe = eng.name == "tensor"
        for b in reads:
            self._need(eng, b.w, same_engine_ok=pe)
        for b in writes:
            self._need(eng, b.w, same_engine_ok=pe)
            for t in b.r:
                self._need(eng, t, same_engine_ok=True)

    @staticmethod
    def _compact(toks):
        best = {}
        for k, v in toks:
            if best.get(k, 0) < v:
                best[k] = v
        return list(best.items())

    def _commit(self, tok, reads, writes):
        for b in reads:
            b.r.append(tok)
            if len(b.r) > 48:
                b.r = self._compact(b.r)
        for b in writes:
            b.w = tok
            b.r = []

    def op(self, engname, emit, reads=(), writes=()):
        eng = self.eng[engname]
        self._deps(eng, reads, writes)
        eng.n += 1
        if engname == "tensor":
            self.pe_labels.append(self.label)
        tok = (eng.sem, eng.n)
        sem = self.sems[eng.sem]
        eng.ops.append(lambda e, emit=emit, sem=sem: emit(e).then_inc(sem, 1))
        self._commit(tok, reads, writes)
        return tok

    def dma(self, qname, semname, emit, reads=(), writes=(), inc=16):
        eng = self.eng[qname]
        self._deps(eng, reads, writes)
        key = self.dma_sem(semname)
        self.dma_cnt[key] += inc
        tok = (key, self.dma_cnt[key])
        sem = self.sems[key]
        eng.ops.append(lambda e, emit=emit, sem=sem, inc=inc: emit(e).then_inc(sem, inc))
        self._commit(tok, reads, writes)
        return tok

    def dma_group(self, qname, semname, emits, reads=(), writes=()):
        eng = self.eng[qname]
        self._deps(eng, reads, writes)
        key = self.dma_sem(semname)
        sem = self.sems[key]
        for emit in emits:
            self.dma_cnt[key] += 16
            eng.ops.append(lambda e, emit=emit, sem=sem: emit(e).then_inc(sem, 16))
        tok = (key, self.dma_cnt[key])
        self._commit(tok, reads, writes)
        return tok

    def wait_all(self, engname, toks):
        eng = self.eng[engname]
        for t in toks:
            self._need(eng, t)

    def emit_all(self):
        nc = self.nc
        with nc.Block() as block:
            for name, dec in (("sync", block.sync), ("gpsimd", block.gpsimd),
                              ("scalar", block.scalar), ("vector", block.vector),
                              ("tensor", block.tensor)):
                ops = self.eng[name].ops
                if not ops:
                    continue

                def body(e, ops=ops):
                    for f in ops:
                        f(e)
                dec(body)


def chunk_tiles(c0, nchunks, per=7):
    nt = (nchunks + per - 1) // per
    base, rem = divmod(nchunks, nt)
    out = []
    col = c0
    for i in range(nt):
        n = base + (1 if i < rem else 0)
        out.append((col, n))
        col += n * 64
    return out


def build_program(stop_after="full", debug=False):
    nc = bass.Bass("TRN2", target_bir_lowering=False)
    dbg = nc.dram_tensor("dbg", [128, 16 * 512], F32, kind="ExternalOutput").ap() if debug else None
    dram = lambda name, shape, kind: nc.dram_tensor(name, shape, F32, kind=kind).ap()
    xT = dram("xT", [D, EXT], "ExternalInput")
    pp = dram("pp", [128, L * NPP], "ExternalInput")
    cst = dram("cst", [128, NCST], "ExternalInput")
    w_in = dram("w_in", [L, D, INC], "ExternalInput")
    pool_w = dram("pool_w", [L, 4, 128, 128], "ExternalInput")
    w_pa = dram("w_pa", [L, 512, D], "ExternalInput")
    w_pb = dram("w_pb", [L, 512, D], "ExternalInput")
    w_o = dram("w_o", [L, D, D], "ExternalInput")
    w_up = dram("w_up", [L, D, 2 * DFF], "ExternalInput")
    w_down = dram("w_down", [L, DFF, D], "ExternalInput")
    outT = dram("outT", [D, OWN], "ExternalOutput")
    cc_in = [nc.dram_tensor("cc_in%d" % l, [128, 516], F32) for l in range(L)]
    cc_out = [nc.dram_tensor("cc_out%d" % l, [NCORE * 128, 516], F32) for l in range(L)]

    with ExitStack() as st:
        P = Prog(nc, st)
        sb = lambda name, shape, dt: st.enter_context(nc.sbuf_tensor(name, shape, dt))

        def mk(engname):
            def f(name, r=(), w=(), **kw):
                return P.op(engname, lambda e, name=name, kw=kw: getattr(e, name)(**kw), reads=r, writes=w)
            return f
        V, A, G, PE = mk("vector"), mk("scalar"), mk("gpsimd"), mk("tensor")

        def DMA(q, semname, dst, src, r=(), w=(), inc=16):
            return P.dma(q, semname, lambda e, dst=dst, src=src: e.dma_start(out=dst, in_=src),
                         reads=r, writes=w, inc=inc)

        x32 = sb("x32", [128, KC, EXT], F32); bx = P.bufs(KC, "x")
        ring = [sb("ring%d" % i, [128, SLOTB], BF16) for i in range(NSLOT)]
        bring = P.bufs(NSLOT, "ring")
        ppt = sb("ppt", [128, L * NPP], F32); bpp = P.buf("pp")
        cstt = sb("cstt", [128, NCST], F32); bcst = P.buf("cst")
        identb = sb("identb", [128, 128], BF16); bident = P.buf("ident")
        onesb = sb("onesb", [128, 128], BF16); bones = P.buf("ones")
        mhalf = sb("mhalf", [128, 2], F32); bmhalf = P.buf("mhalf")
        hbg = sb("hbg", [128, L * 16], F32); bhbg = P.buf("hbg")
        nbg = sb("nbg", [128, L * 16], F32); bnbg = P.buf("nbg")
        lbc = sb("lbc", [128, 6, L * 4], F32); blbc = P.buf("lbc")
        pwb = sb("pwb", [128, L * 4, 128], BF16); bpwb = P.buf("pwb")
        xn = sb("xn", [128, KC, 2 + TMAX], BF16); bxn = P.bufs(KC, "xn")
        xcar = sb("xcar", [128, KC, 2], BF16); bxcar = P.buf("xcar")
        sqb = sb("sqb", [128, 2, TMAX], BF16); bsqb = P.bufs(2, "sqb")
        rs1 = sb("rs1", [128, TMAX], F32); brs1 = P.buf("rs1")
        rstd = sb("rstd", [128, TMAX], F32); brstd = P.buf("rstd")
        sq = sb("sq", [128, 4, TMAX], BF16); bsq = P.bufs(4, "sq")
        szo = sb("szo", [128, 4, TMAX], BF16); bszo = P.bufs(4, "szo")
        thn = sb("thn", [128, 4, TMAX], F32); bthn = P.bufs(4, "thn")
        vt = sb("vt", [64, 4, 7 * 128], BF16); bvt = P.bufs(4, "vt")
        gG = sb("gG", [128, TMAX], F32); bgG = P.buf("gG")
        gB = sb("gB", [128, TMAX], F32); bgB = P.buf("gB")
        gE = sb("gE", [128, TMAX], F32); bgE = P.buf("gE")
        ebl = sb("ebl", [128, 8], F32); bebl = P.buf("ebl")
        qh = sb("qh", [128, TMAX], BF16); bqh = P.buf("qh")
        kt = sb("kt", [128, TMAX], BF16); bkt = P.buf("kt")
        kh = sb("kh", [128, TMAX], BF16); bkh = P.buf("kh")
        khtok = sb("khtok", [64, 7 * 128], BF16); bkhtok = P.buf("khtok")
        attnm = sb("attnm", [64, TMAX], BF16); battnm = P.buf("attnm")
        S32 = sb("S32", [128, 4, 128], F32); bS32 = P.bufs(4, "S32")
        Sbf = sb("Sbf", [128, 7, 128], BF16); bSbf = P.bufs(7, "Sbf")
        bsum = sb("bsum", [128, 8], F32); bbsum = P.buf("bsum")
        osq = sb("osq", [128, TMAX], BF16); bosq = P.buf("osq")
        onf = sb("onf", [128, TMAX], F32); bonf = P.buf("onf")
        onb = sb("onb", [128, 4, TMAX], BF16); bonb = P.bufs(4, "onb")
        ccs = sb("ccs", [128, 516], F32); bccs = P.buf("ccs")
        ccr = sb("ccr", [128, 516], F32); bccr = P.buf("ccr")
        agt = sb("agt", [128, 520], F32); bagt = P.buf("agt")
        pcar = sb("pcar", [128, 4, 16], F32); bpcar = P.buf("pcar")
        ut = sb("ut", [128, 1, 16 + TMAX], F32); but = P.bufs(1, "ut")
        psA = sb("psA", [128, 16 + TMAX], F32); bpsA = P.buf("psA")
        psB = sb("psB", [128, 16 + TMAX], F32); bpsB = P.buf("psB")
        pooled = sb("pooled", [128, 4, TMAX], BF16); bpooled = P.bufs(4, "pooled")
        mixed = sb("mixed", [128, 4, TMAX], BF16); bmixed = P.bufs(4, "mixed")
        tht = sb("tht", [128, 2, TMAX], F32); btht = P.bufs(2, "tht")
        At = sb("At", [128, TMAX], F32); bAt = P.buf("At")
        Bt = sb("Bt", [128, TMAX], F32); bBt = P.buf("Bt")
        Mb = sb("Mb", [128, KC, TMAX], BF16); bMb = P.bufs(KC, "Mb")
        cv, bcv = gG, bgG
        cg, bcg = gB, bgB
        sg, bsg = gE, bgE
        aT = sb("aT", [128, 12, TMAX], BF16); baT = P.bufs(12, "aT")
        ostg = [At, Bt]; bostg = [bAt, bBt]
        print("SBUF bytes remaining per partition:", nc.sbuf_bytes_remaining)

        dbgst = sb("dbgst", [128, 512], F32) if debug else None
        bdbg = P.buf("dbg")
        dbg_names = []

        def dump(name, src, rbufs, ncol, parts=128):
            if not debug or len(dbg_names) >= 16:
                return
            i = len(dbg_names)
            dbg_names.append(name)
            V("memset", w=[bdbg], ap=dbgst[:], constant=0.0)
            V("tensor_copy", r=list(rbufs), w=[bdbg], out=dbgst[0:parts, 0:ncol], in_=src)
            out_toks.append(DMA("sync", "dbgst", dbg[:, i * 512:(i + 1) * 512], dbgst[:], r=[bdbg]))
        build_program.dbg_names = dbg_names

        pbank = [st.enter_context(nc.psum_tensor("pb%d" % i, [128, 512], F32)) for i in range(7)]
        bpb = P.bufs(7, "pb")
        ptr = st.enter_context(nc.psum_tensor("ptr", [128, 1024], BF16)); bptr = P.buf("ptr")
        state = {"mm": 0, "mmset": [0, 1, 2]}

        def mmbank():
            s = state["mmset"]
            i = s[state["mm"] % len(s)]
            state["mm"] += 1
            return pbank[i], bpb[i]

        seq = []

        def v3(slot, k, n):
            return ring[slot][:, 0:k * n].rearrange("p (k n) -> p k n", k=k)

        def src_kn(ap2d, r0, nk, c0, ncol):
            return ap2d[r0:r0 + nk * 128, c0:c0 + ncol].rearrange("(k p) n -> p k n", p=128)

        issued = {"n": 0}

        wscr = nc.dram_tensor("wscr", [80, 128, SLOTB], BF16)
        keyslot = {}
        bscr = {}

        def live(i):
            while issued["n"] < len(seq) and issued["n"] <= i + NSLOT - 1:
                j = issued["n"]
                slot = j % NSLOT
                key, nelem, parts = seq[j]
                if key is None or key not in keyslot:
                    emits = [(lambda e, dst=dst_fn(slot), src=src: e.dma_start(out=dst, in_=src))
                             for dst_fn, src in parts]
                    P.dma_group("gpsimd", "ring%d" % slot, emits, writes=[bring[slot]])
                    if key is not None and REUSE_BF16:
                        k = len(keyslot)
                        keyslot[key] = k
                        bscr[key] = P.buf("scr%d" % k)
                        DMA("sync", "scrst%d" % (k % 4), wscr.ap()[k, :, 0:nelem], ring[slot][:, 0:nelem],
                            r=[bring[slot]], w=[bscr[key]])
                else:
                    k = keyslot[key]
                    DMA("sync", "ringl%d" % slot, ring[slot][:, 0:nelem], wscr.ap()[k, :, 0:nelem],
                        r=[bscr[key]], w=[bring[slot]])
                issued["n"] += 1

        def use_slab(i, first=True, min_live=None):
            if min_live is not None:
                live(min_live)
            elif first:
                live(i)
            assert issued["n"] > i, (issued["n"], i)
            return i % NSLOT

        def add_slab(parts, nelem=SLOTB, key=None):
            seq.append((key, nelem, parts))
            return len(seq) - 1

        def slab_H(l, h):
            parts = []
            for j, base in enumerate((512, 1024, 2048, 1536)):
                parts.append((lambda s, j=j: v3(s, 8, 512)[:, :, j * 128:(j + 1) * 128],
                              src_kn(w_in[l], 0, 8, base + h * 128, 128)))
            return add_slab(parts, 4096, ("H", l, h))

        def slab_full(ap2d, c0, key=None):
            return add_slab([(lambda s: v3(s, 8, 512), src_kn(ap2d, 0, 8, c0, 512))], 4096, key)

        def slab_pair(ap2d, ca, cb, key=None):
            return add_slab([(lambda s: v3(s, 8, 512)[:, :, 0:256], src_kn(ap2d, 0, 8, ca, 256)),
                             (lambda s: v3(s, 8, 512)[:, :, 256:512], src_kn(ap2d, 0, 8, cb, 256))], 4096, key)

        def slab_P(l, i):
            return add_slab([(lambda s: v3(s, 8, 256)[:, 0:4, :], src_kn(w_pa[l], 0, 4, i * 256, 256)),
                             (lambda s: v3(s, 8, 256)[:, 4:8, :], src_kn(w_pb[l], 0, 4, i * 256, 256))],
                            2048, ("P", l, i))

        def slab_D(l, k0, nk, q):
            return add_slab([(lambda s, nk=nk: v3(s, nk, 256), src_kn(w_down[l], k0 * 128, nk, q * 256, 256))],
                            nk * 256, ("D", l, k0, q))

        out_toks = []
        stages = ["l0mix", "l0ffn", "l1mix", "l1ffn", "full"]
        stop_i = stages.index(stop_after)

        plan = []
        for l in range(L):
            if 2 * l > stop_i:
                break
            E0 = 0
            p1 = chunk_tiles(E0, OWN // 64 + (1 if (l == 1 and EXACT_L1) else 0))
            p2 = chunk_tiles(E0, (EXT - E0) // 64)
            ids1 = {"F": slab_full(w_in[l], 1024), "I": slab_full(w_in[l], 1536)}
            plan.append(("p1", l, p1, ids1))
            do_ffn = stop_i >= 2 * l + 1
            tiles2 = []
            for ti, (c0, nC) in enumerate(p2):
                ids = {}
                ids["H"] = [slab_H(l, h) for h in range(4)]
                ids["Wp"] = slab_full(w_in[l], 0, ("Wp", l))
                ids["PA"] = add_slab([(lambda s: v3(s, 4, 1024), src_kn(w_pa[l], 0, 4, 0, 1024))], 4096, ("PA", l))
                ids["G0"] = [slab_full(w_in[l], 2560 + i * 512, ("G0", l, i)) for i in range(2)]
                ids["PB"] = add_slab([(lambda s: v3(s, 4, 1024), src_kn(w_pb[l], 0, 4, 0, 1024))], 4096, ("PB", l))
                ids["G1"] = [slab_full(w_in[l], 3584 + i * 512, ("G1", l, i)) for i in range(2)]
                ids["O"] = [slab_full(w_o[l], i * 512, ("O", l, i)) for i in range(2)]
                if do_ffn:
                    ids["U0"] = [slab_pair(w_up[l], s * 256, DFF + s * 256, ("U", l, s)) for s in range(6)]
                    ids["D0"] = [slab_D(l, 0, 12, q) for q in range(4)]
                    ids["U1"] = [slab_pair(w_up[l], s * 256, DFF + s * 256, ("U", l, s)) for s in range(6, 11)]
                    ids["D1"] = [slab_D(l, 12, 10, q) for q in range(4)]
                tiles2.append((c0, nC, ids))
            plan.append(("p2", l, tiles2, do_ffn))

        for kc in range(KC):
            DMA("sync", "ldx%d" % kc, x32[:, kc, :], xT[kc * 128:(kc + 1) * 128, :], w=[bx[kc]])
        DMA("sync", "ldpp", ppt[:], pp, w=[bpp])
        DMA("sync", "ldcst", cstt[:], cst, w=[bcst])
        DMA("gpsimd", "ldpw", pwb[:], pool_w.rearrange("l g c d -> c (l g) d"), w=[bpwb])
        V("tensor_copy", r=[bcst], w=[bident], out=identb[:], in_=cstt[:, 1024:1152])
        V("memset", w=[bones], ap=onesb[:], constant=1.0)
        V("memset", w=[bmhalf], ap=mhalf[:], constant=EPS)
        V("memset", w=[bmhalf], ap=mhalf[:, 1:2], constant=1.0)
        V("memset", w=[bxcar], ap=xcar[:], constant=0.0)
        for t_, b_ in ((psA, bpsA), (psB, bpsB), (gG, bgG), (gB, bgB), (gE, bgE), (rs1, brs1), (rstd, brstd),
                       (At, bAt), (Bt, bBt), (onf, bonf), (agt, bagt), (ccr, bccr), (ccs, bccs), (ebl, bebl),
                       (bsum, bbsum)):
            V("memset", w=[b_], ap=t_[:], constant=0.0)
        for t_, bl_ in ((ut, but), (tht, btht)):
            G("memset", w=list(bl_), ap=t_[:], constant=0.0)
        for t_, b_ in ((kt, bkt), (kh, bkh), (qh, bqh), (osq, bosq), (attnm, battnm), (khtok, bkhtok)):
            G("memset", w=[b_], ap=t_[:], constant=0.0)
        for t_, bl_ in ((sqb, bsqb), (xn, bxn), (sq, bsq), (szo, bszo), (thn, bthn), (vt, bvt), (onb, bonb),
                        (pooled, bpooled), (mixed, bmixed), (Mb, bMb), (aT, baT), (Sbf, bSbf)):
            G("memset", w=list(bl_), ap=t_[:], constant=0.0)
        for l in range(L):
            V("tensor_scalar", r=[bpp], w=[bhbg], out=hbg[:, l * 16:(l + 1) * 16],
              in0=ppt[:, l * NPP + 8:l * NPP + 24], scalar1=0.5, scalar2=None, op0=ALU.mult)
        for l in range(L):
            V("tensor_scalar", r=[bpp], w=[bnbg], out=nbg[:, l * 16:(l + 1) * 16],
              in0=ppt[:, l * NPP + 8:l * NPP + 24], scalar1=-1.0, scalar2=None, op0=ALU.mult)
        V("memset", w=[blbc], ap=lbc[:], constant=0.0)
        V("tensor_tensor", r=[bpp, blbc], w=[blbc], out=lbc[:, 4, 0:4], in0=ppt[:, 32:36], in1=ppt[:, 28:32],
          op=ALU.subtract)
        A("activation", r=[blbc], w=[blbc], out=lbc[:, 5, 0:4], in_=lbc[:, 4, 0:4], func=AF.Tanh, scale=0.5)
        V("tensor_scalar", r=[blbc], w=[blbc], out=lbc[:, 0, 4:8], in0=lbc[:, 5, 0:4], scalar1=0.5, scalar2=0.5,
          op0=ALU.mult, op1=ALU.add)
        V("tensor_scalar", r=[blbc], w=[blbc], out=lbc[:, 1, :], in0=lbc[:, 0, :], scalar1=-0.5, scalar2=0.5,
          op0=ALU.mult, op1=ALU.add)
        V("tensor_scalar", r=[blbc], w=[blbc], out=lbc[:, 2, :], in0=lbc[:, 0, :], scalar1=0.5, scalar2=0.5,
          op0=ALU.mult, op1=ALU.add)
        V("tensor_scalar", r=[blbc], w=[blbc], out=lbc[:, 3, :], in0=lbc[:, 0, :], scalar1=0.5, scalar2=-0.5,
          op0=ALU.mult, op1=ALU.add)

        def norm_rstd(c0, T, dim):
            P.label = "norm"
            bank, bb = mmbank()
            for kc in range(KC):
                j = kc % 2
                A("activation", r=[bx[kc]], w=[bsqb[j]], out=sqb[:, j, 0:T], in_=x32[:, kc, c0:c0 + T],
                  func=AF.Square)
                PE("matmul", r=[bsqb[j], bones], w=[bb], out=bank[:, 0:T], lhsT=onesb[:], rhs=sqb[:, j, 0:T],
                   start=(kc == 0), stop=(kc == KC - 1))
            A("activation", r=[bb, bmhalf], w=[brs1], out=rs1[:, 0:T], in_=bank[:, 0:T], func=AF.Ln,
              scale=1.0 / dim, bias=mhalf[:, 0:1])
            A("activation", r=[brs1], w=[brstd], out=rstd[:, 0:T], in_=rs1[:, 0:T], func=AF.Exp, scale=-0.5)

        def emit_norm(c0, T, gbase):
            norm_rstd(c0, T, D)
            for kc in range(KC):
                V("scalar_tensor_tensor", r=[bx[kc], bpp, brstd], w=[bxn[kc]], out=xn[:, kc, 2:2 + T],
                  in0=x32[:, kc, c0:c0 + T], scalar=ppt[:, gbase + kc:gbase + kc + 1], in1=rstd[:, 0:T],
                  op0=ALU.mult, op1=ALU.mult)

        def proj_chunk(slot, colsl, T, xoff=2):
            bank, bb = mmbank()
            W = v3(slot, 8, 512)
            for kc in range(KC):
                PE("matmul", r=[bring[slot], bxn[kc]], w=[bb], out=bank[:, 0:T], lhsT=W[:, kc, colsl],
                   rhs=xn[:, kc, xoff:xoff + T], start=(kc == 0), stop=(kc == KC - 1))
            return bank, bb

        def proj_v(slot, colsl, nC, h):
            W = v3(slot, 8, 512)
            for c_lo in range(0, nC, 4):
                bank, bb = mmbank()
                n = min(4, nC - c_lo)
                for cc in range(n):
                    c = c_lo + cc
                    for kc in range(KC):
                        PE("matmul", r=[bring[slot], bxn[kc]], w=[bb], out=bank[0:64, cc * 128:(cc + 1) * 128],
                           lhsT=xn[:, kc, 2 + c * 64:2 + (c + 1) * 64], rhs=W[:, kc, colsl],
                           start=(kc == 0), stop=(kc == KC - 1))
                A("activation", r=[bb], w=[bvt[h]], out=vt[:, h, c_lo * 128:(c_lo + n) * 128],
                  in_=bank[0:64, 0:n * 128], func=AF.Identity)

        ebl2 = sb("ebl2", [128, 8], F32); bebl2 = P.buf("ebl2")
        V("memset", w=[bebl2], ap=ebl2[:], constant=0.0)
        HS = [dict(gG=gG, bgG=bgG, gB=gB, bgB=bgB, gE=gE, bgE=bgE, kt=kt, bkt=bkt, kh=kh, bkh=bkh, qh=qh, bqh=bqh,
                   ebl=ebl, bebl=bebl),
              dict(gG=tht[:, 0, :], bgG=btht[0], gB=tht[:, 1, :], bgB=btht[1], gE=At, bgE=bAt,
                   kt=Mb[:, 0, :], bkt=bMb[0], kh=Mb[:, 1, :], bkh=bMb[1], qh=Mb[:, 2, :], bqh=bMb[2],
                   ebl=ebl2, bebl=bebl2)]

        def hgrn_s1(l, h, nC, full, neutral0, S):
            P.label = "hstate%d" % (2 if full else 1)
            T = nC * 64
            li = l * 4 + h
            gG_, gB_, gE_, kt_, kh_, qh_, ebl_ = S["gG"], S["gB"], S["gE"], S["kt"], S["kh"], S["qh"], S["ebl"]
            A("activation", r=[bthn[h], blbc], w=[S["bgG"]], out=gG_[:, 0:T], in_=thn[:, h, 0:T], func=AF.Ln,
              scale=lbc[:, 3, li:li + 1], bias=lbc[:, 2, li:li + 1])
            if neutral0 and EXACT_L1:
                V("memset", w=[S["bgG"]], ap=gG_[:, 0:64], constant=0.0)
            V("tensor_tensor_scan", r=[S["bgG"], bcst], w=[S["bgB"]], out=gB_[:, 0:T], data0=cstt[:, 0:T],
              data1=gG_[:, 0:T], initial=0.0, op0=ALU.mult, op1=ALU.add)
            A("activation", r=[S["bgB"]], w=[S["bgE"]], out=gE_[:, 0:T], in_=gB_[:, 0:T], func=AF.Exp, scale=-1.0)
            A("activation", r=[S["bgB"]], w=[S["bebl"]], out=ebl_[:, 0:nC], in_=gB_[:, 63:T:64], func=AF.Exp)
            V("scalar_tensor_tensor", r=[bthn[h], S["bgE"]], w=[S["bkt"]], out=kt_[:, 0:T], in0=thn[:, h, 0:T],
              scalar=1.0, in1=gE_[:, 0:T], op0=ALU.add, op1=ALU.mult)
            V("tensor_tensor", r=[S["bkt"], S["bebl"]], w=[S["bkh"]],
              out=kh_[:, 0:T].rearrange("p (c t) -> p c t", t=64),
              in0=kt_[:, 0:T].rearrange("p (c t) -> p c t", t=64),
              in1=ebl_[:, 0:nC].unsqueeze(2).to_broadcast([128, nC, 64]), op=ALU.mult)
            if full:
                A("activation", r=[S["bgB"]], w=[S["bgG"]], out=gG_[:, 0:T], in_=gB_[:, 0:T], func=AF.Exp)
                V("scalar_tensor_tensor", r=[bsq[h], blbc, S["bgG"]], w=[S["bqh"]], out=qh_[:, 0:T],
                  in0=sq[:, h, 0:T], scalar=lbc[:, 1, li:li + 1], in1=gG_[:, 0:T], op0=ALU.mult, op1=ALU.mult)
            else:
                V("tensor_reduce", r=[S["bgB"]], w=[bbsum], out=bsum[:, 4 + h:5 + h], in_=gB_[:, 63:T:64],
                  axis=mybir.AxisListType.X, op=ALU.add)
                V("tensor_tensor", r=[bbsum], w=[bbsum], out=bsum[:, h:h + 1], in0=bsum[:, h:h + 1],
                  in1=bsum[:, 4 + h:5 + h], op=ALU.add)

        def hgrn_s2(l, h, nC, full, S, neutral0=False):
            P.label = "hstate%d" % (2 if full else 1)
            kh_, ebl_ = S["kh"], S["ebl"]
            if neutral0 and EXACT_L1:
                V("memset", w=[bvt[h]], ap=vt[:, h, 0:128], constant=0.0)
            for c in range(nC):
                PE("transpose", r=[S["bkh"], bident], w=[bptr], out=ptr[0:64, c * 128:(c + 1) * 128],
                   in_=kh_[:, c * 64:(c + 1) * 64], identity=identb[:])
            A("activation", r=[bptr], w=[bkhtok], out=khtok[:, 0:nC * 128], in_=ptr[0:64, 0:nC * 128],
              func=AF.Identity)
            for c in range(nC):
                bi = 4 + c // 4
                PE("matmul", r=[bkhtok, bvt[h]], w=[bpb[bi]], out=pbank[bi][:, (c % 4) * 128:(c % 4 + 1) * 128],
                   lhsT=khtok[:, c * 128:(c + 1) * 128], rhs=vt[:, h, c * 128:(c + 1) * 128], start=True, stop=True)
            for c in range(nC):
                bi = 4 + c // 4
                if full:
                    V("tensor_copy", r=[bS32[h]], w=[bSbf[c]], out=Sbf[:, c, :], in_=S32[:, h, :])
                V("scalar_tensor_tensor", r=[bS32[h], S["bebl"], bpb[bi]], w=[bS32[h]], out=S32[:, h, :],
                  in0=S32[:, h, :], scalar=ebl_[:, c:c + 1], in1=pbank[bi][:, (c % 4) * 128:(c % 4 + 1) * 128],
                  op0=ALU.mult, op1=ALU.add)

        def hgrn_out(l, h, nC, S):
            P.label = "hout"
            T = nC * 64
            kt, bkt, qh, bqh = S["kt"], S["bkt"], S["qh"], S["bqh"]
            pat, bpat = pbank[3], bpb[3]
            for c in range(nC):
                PE("matmul", r=[bkt, bqh], w=[bpat], out=pat[0:64, c * 64:(c + 1) * 64],
                   lhsT=kt[:, c * 64:(c + 1) * 64], rhs=qh[:, c * 64:(c + 1) * 64], start=True, stop=True)
            V("tensor_tensor", r=[bpat, bcst], w=[battnm], out=attnm[:, 0:T], in0=pat[0:64, 0:T],
              in1=cstt[0:64, 512:512 + T], op=ALU.mult)
            po, bpo = pbank[6], bpb[6]
            for c in range(nC):
                PE("matmul", r=[bvt[h], battnm], w=[bpo], out=po[:, c * 64:(c + 1) * 64],
                   lhsT=vt[:, h, c * 128:(c + 1) * 128], rhs=attnm[:, c * 64:(c + 1) * 64], start=True, stop=False)
                PE("matmul", r=[bSbf[c], bqh], w=[bpo], out=po[:, c * 64:(c + 1) * 64], lhsT=Sbf[:, c, :],
                   rhs=qh[:, c * 64:(c + 1) * 64], start=False, stop=True)
            A("activation", r=[bpo], w=[bosq], out=osq[:, 0:T], in_=po[:, 0:T], func=AF.Square)
            PE("matmul", r=[bosq, bones], w=[bpat], out=pat[:, 0:T], lhsT=onesb[:], rhs=osq[:, 0:T],
               start=True, stop=True)
            A("activation", r=[bpat, bmhalf], w=[brs1], out=rs1[:, 0:T], in_=pat[:, 0:T], func=AF.Ln,
              scale=1.0 / 128, bias=mhalf[:, 0:1])
            A("activation", r=[brs1], w=[brstd], out=rstd[:, 0:T], in_=rs1[:, 0:T], func=AF.Exp, scale=-0.5)
            gcol = l * NPP + 36
            V("scalar_tensor_tensor", r=[bpo, bpp, brstd], w=[bonf], out=onf[:, 0:T], in0=po[:, 0:T],
              scalar=ppt[:, gcol:gcol + 1], in1=rstd[:, 0:T], op0=ALU.mult, op1=ALU.mult)
            V("tensor_tensor", r=[bonf, bszo[h]], w=[bonb[h]], out=onb[:, h, 0:T], in0=onf[:, 0:T],
              in1=szo[:, h, 0:T], op=ALU.mult)

        def pass1_tile(l, c0, nC, ids):
            T = nC * 64
            emit_norm(c0, T, l * NPP + 0)
            P.label = "p1proj"
            sF = use_slab(ids["F"])
            sI = use_slab(ids["I"], first=False)
            for h in range(4):
                bank, bb = proj_chunk(sF, slice(h * 128, (h + 1) * 128), T)
                A("activation", r=[bb], w=[bthn[h]], out=thn[:, h, 0:T], in_=bank[:, 0:T], func=AF.Tanh, scale=-0.5)
            n0 = (l == 1 and c0 == 0)

            def vproj(h):
                P.label = "p1proj"
                proj_v(sI, slice(h * 128, (h + 1) * 128), nC, h)
            hgrn_s1(l, 0, nC, False, n0, HS[0])
            hgrn_s1(l, 1, nC, False, n0, HS[1])
            vproj(0)
            vproj(1)
            hgrn_s2(l, 0, nC, False, HS[0], n0)
            vproj(2)
            hgrn_s1(l, 2, nC, False, n0, HS[0])
            hgrn_s2(l, 1, nC, False, HS[1], n0)
            vproj(3)
            hgrn_s1(l, 3, nC, False, n0, HS[1])
            hgrn_s2(l, 2, nC, False, HS[0], n0)
            hgrn_s2(l, 3, nC, False, HS[1], n0)

        def pass1(l, tiles, ids):
            for h in range(4):
                V("memset", w=[bS32[h]], ap=S32[:, h, :], constant=0.0)
            V("memset", w=[bbsum], ap=bsum[:], constant=0.0)
            for (c0, nC) in tiles:
                pass1_tile(l, c0, nC, ids)
            for h in range(4):
                V("tensor_copy", r=[bS32[h]], w=[bccs], out=ccs[:, h * 128:(h + 1) * 128], in_=S32[:, h, :])
            A("activation", r=[bbsum], w=[bccs], out=ccs[:, 512:516], in_=bsum[:, 0:4], func=AF.Exp)

        def exchange(l):
            bcci = P.buf("cci"); bcco = P.buf("cco")
            DMA("sync", "ccst", cc_in[l].ap(), ccs[:], r=[bccs], w=[bcci])
            cin = cc_in[l].ap().opt()
            cout = cc_out[l].ap().opt()
            P.dma("gpsimd", "cc",
                  lambda e, cin=cin, cout=cout: e.collective_compute(
                      "AllGather", ALU.bypass, replica_groups=[list(range(NCORE))], ins=[cin], outs=[cout]),
                  reads=[bcci], writes=[bcco], inc=1)
            for h in range(4):
                V("memset", w=[bS32[h]], ap=S32[:, h, :], constant=0.0)
            for r in range(NCORE):
                DMA("sync", "ccld", ccr[:], cc_out[l].ap()[r * 128:(r + 1) * 128, :], r=[bcco], w=[bccr])
                am = cstt[:, 1217 + r:1218 + r]
                nam = cstt[:, 1225 + r:1226 + r]
                V("tensor_scalar", r=[bccr, bcst], w=[bagt], out=agt[:, 512:516], in0=ccr[:, 512:516],
                  scalar1=am, scalar2=nam, op0=ALU.mult, op1=ALU.add)
                V("tensor_scalar", r=[bccr, bcst], w=[bagt], out=agt[:, 0:512], in0=ccr[:, 0:512], scalar1=am,
                  scalar2=None, op0=ALU.mult)
                for h in range(4):
                    V("scalar_tensor_tensor", r=[bS32[h], bagt], w=[bS32[h]], out=S32[:, h, :], in0=S32[:, h, :],
                      scalar=agt[:, 512 + h:513 + h], in1=agt[:, h * 128:(h + 1) * 128], op0=ALU.mult, op1=ALU.add)

        def mixer_tile(l, ti, c0, nC, ids, have_norm=False):
            T = nC * 64
            pb_ = l * NPP
            if not have_norm:
                emit_norm(c0, T, pb_ + 0)
            dd = (l == 0 and ti == 1)
            if dd:
                dump("xn0", xn[:, 0, 2:2 + T], [bxn[0]], T)
                dump("rstd", rstd[:, 0:T], [brstd], T)
            P.label = "Hproj"
            sH = []
            for h in range(4):
                s = use_slab(ids["H"][h], min_live=ids["H"][0])
                sH.append(s)
                bank, bb = proj_chunk(s, slice(0, 128), T)
                A("activation", r=[bb], w=[bsq[h]], out=sq[:, h, 0:T], in_=bank[:, 0:T], func=AF.Silu)
                bank, bb = proj_chunk(s, slice(128, 256), T)
                A("activation", r=[bb], w=[bthn[h]], out=thn[:, h, 0:T], in_=bank[:, 0:T], func=AF.Tanh, scale=-0.5)
                bank, bb = proj_chunk(s, slice(256, 384), T)
                A("activation", r=[bb], w=[bszo[h]], out=szo[:, h, 0:T], in_=bank[:, 0:T], func=AF.Silu)
            n0 = (l == 1 and c0 == 0)
            W = 16 + T

            def vproj(h):
                P.label = "Hproj"
                proj_v(sH[h], slice(384, 512), nC, h)
                live(ids["H"][h] + 1)

            def pool_group(g, min_live):
                P.label = "pool"
                s = use_slab(ids["Wp"], min_live=min_live)
                j = 0
                w = 2 << g
                bank, bb = proj_chunk(s, slice(g * 128, (g + 1) * 128), T)
                V("tensor_copy", r=[bpcar], w=[but[j]], out=ut[:, j, 0:16], in_=pcar[:, g, :])
                A("activation", r=[bb], w=[but[j]], out=ut[:, j, 16:W], in_=bank[:, 0:T], func=AF.Identity)
                src, bsrc = ut[:, j, :], but[j]
                tmps = [(psA, bpsA), (psB, bpsB)]
                for lev in range(g + 1):
                    sh = 1 << lev
                    dstt, bdst = tmps[lev % 2]
                    V("tensor_tensor", r=[bsrc], w=[bdst], out=dstt[:, sh:W], in0=src[:, sh:W], in1=src[:, 0:W - sh],
                      op=ALU.add)
                    src, bsrc = dstt, bdst
                V("scalar_tensor_tensor", r=[bsrc, but[j]], w=[bpooled[g]], out=pooled[:, g, 0:T], in0=src[:, 16:W],
                  scalar=1.0 / w, in1=ut[:, j, 16:W], op0=ALU.mult, op1=ALU.subtract)
                if c0 <= 128 and c0 + T >= 144:
                    o = 128 - c0
                    V("tensor_tensor", r=[bsrc, bcst], w=[brs1], out=rs1[:, 0:16], in0=src[:, 16 + o:32 + o],
                      in1=cstt[:, 1152 + g * 16:1168 + g * 16], op=ALU.mult)
                    V("tensor_tensor", r=[brs1, but[j]], w=[bpooled[g]], out=pooled[:, g, o:o + 16], in0=rs1[:, 0:16],
                      in1=ut[:, j, 16 + o:32 + o], op=ALU.subtract)
                V("tensor_copy", r=[but[j]], w=[bpcar], out=pcar[:, g, :], in_=ut[:, j, T:T + 16])
                bank2, bb2 = mmbank()
                PE("matmul", r=[bpwb, bpooled[g]], w=[bb2], out=bank2[:, 0:T], lhsT=pwb[:, l * 4 + g, :],
                   rhs=pooled[:, g, 0:T], start=True, stop=True)
                A("activation", r=[bb2, bpp], w=[bmixed[g]], out=mixed[:, g, 0:T], in_=bank2[:, 0:T],
                  func=AF.Identity, scale=ppt[:, pb_ + 24 + g:pb_ + 25 + g])

            if ti == 0:
                V("memset", w=[bpcar], ap=pcar[:], constant=0.0)
            def gate0(m):
                P.label = "gates"
                sPA = use_slab(ids["PA"], min_live=ids["PA"])
                sG = use_slab(ids["G0"][m // 4], first=False)
                WPA = v3(sPA, 4, 1024)
                bank, bb = proj_chunk(sG, slice((m % 4) * 128, (m % 4 + 1) * 128), T)
                A("activation", r=[bb, bnbg], w=[bpsA], out=psA[:, 0:T], in_=bank[:, 0:T], func=AF.Exp, scale=-1.0,
                  bias=nbg[:, l * 16 + m:l * 16 + m + 1])
                A("activation", r=[bpsA, bmhalf], w=[bpsB], out=psB[:, 0:T], in_=psA[:, 0:T], func=AF.Ln,
                  bias=mhalf[:, 1:2])
                A("activation", r=[bpsB], w=[bpsA], out=psA[:, 0:T], in_=psB[:, 0:T], func=AF.Exp, scale=-1.0)
                bya, bbya = mmbank()
                for g in range(4):
                    PE("matmul", r=[bring[sPA], bmixed[g]], w=[bbya], out=bya[:, 0:T],
                       lhsT=WPA[:, g, m * 128:(m + 1) * 128], rhs=mixed[:, g, 0:T], start=(g == 0), stop=(g == 3))
                V("tensor_tensor", r=[bpsA, bbya], w=[baT[m]], out=aT[:, m, 0:T], in0=psA[:, 0:T], in1=bya[:, 0:T],
                  op=ALU.mult)

            hgrn_s1(l, 0, nC, True, n0, HS[0])
            hgrn_s1(l, 1, nC, True, n0, HS[1])
            vproj(0)
            vproj(1)
            pool_group(0, ids["H"][2])
            pool_group(1, ids["H"][2])
            hgrn_s2(l, 0, nC, True, HS[0], n0)
            hgrn_out(l, 0, nC, HS[0])
            vproj(2)
            hgrn_s1(l, 2, nC, True, n0, HS[0])
            pool_group(2, ids["H"][3])
            pool_group(3, ids["H"][3])
            hgrn_s2(l, 1, nC, True, HS[1], n0)
            hgrn_out(l, 1, nC, HS[1])
            vproj(3)
            hgrn_s1(l, 3, nC, True, n0, HS[1])
            live(ids["Wp"])
            gate0(0)
            gate0(1)
            gate0(2)
            hgrn_s2(l, 2, nC, True, HS[0], n0)
            hgrn_out(l, 2, nC, HS[0])
            gate0(3)
            gate0(4)
            gate0(5)
            hgrn_s2(l, 3, nC, True, HS[1], n0)
            hgrn_out(l, 3, nC, HS[1])
            gate0(6)
            gate0(7)
            P.label = "gates"
            for i in range(2):
                sPB = use_slab(ids["PB"], min_live=ids["PB"])
                sG = use_slab(ids["G1"][i], first=False)
                WPB = v3(sPB, 4, 1024)
                for mm in range(4):
                    m = 4 * i + mm
                    bank, bb = proj_chunk(sG, slice(mm * 128, (mm + 1) * 128), T)
                    A("activation", r=[bb, bhbg], w=[btht[1]], out=tht[:, 1, 0:T], in_=bank[:, 0:T], func=AF.Tanh,
                      scale=0.5, bias=hbg[:, l * 16 + 8 + m:l * 16 + 9 + m])
                    byb, bbyb = mmbank()
                    for h in range(4):
                        PE("matmul", r=[bring[sPB], bonb[h]], w=[bbyb], out=byb[:, 0:T],
                           lhsT=WPB[:, h, m * 128:(m + 1) * 128], rhs=onb[:, h, 0:T], start=(h == 0), stop=(h == 3))
                    V("scalar_tensor_tensor", r=[btht[1], bbyb], w=[bBt], out=Bt[:, 0:T], in0=tht[:, 1, 0:T],
                      scalar=1.0, in1=byb[:, 0:T], op0=ALU.add, op1=ALU.mult)
                    V("scalar_tensor_tensor", r=[baT[m], bBt], w=[bMb[m]], out=Mb[:, m, 0:T], in0=aT[:, m, 0:T],
                      scalar=2.0, in1=Bt[:, 0:T], op0=ALU.mult, op1=ALU.add)
            P.label = "wo"
            for i in range(2):
                sO = use_slab(ids["O"][i])
                WO = v3(sO, 8, 512)
                for mm in range(4):
                    m = 4 * i + mm
                    bank, bb = mmbank()
                    for k in range(KC):
                        PE("matmul", r=[bring[sO], bMb[k]], w=[bb], out=bank[:, 0:T],
                           lhsT=WO[:, k, mm * 128:(mm + 1) * 128], rhs=Mb[:, k, 0:T], start=(k == 0), stop=(k == KC - 1))
                    V("scalar_tensor_tensor", r=[bb, bx[m]], w=[bx[m]], out=x32[:, m, c0:c0 + T], in0=bank[:, 0:T],
                      scalar=0.5, in1=x32[:, m, c0:c0 + T], op0=ALU.mult, op1=ALU.add)

        def ffn_tile(l, ti, c0, nC, ids, nxt=None):
            T = nC * 64
            pb_ = l * NPP
            if ti == 0:
                V("memset", w=[bxcar], ap=xcar[:], constant=0.0)
            emit_norm(c0, T, pb_ + 37)
            V("tensor_copy", r=[bxcar], w=bxn, out=xn[:, :, 0:2], in_=xcar[:])
            V("tensor_copy", r=bxn, w=[bxcar], out=xcar[:], in_=xn[:, :, T:T + 2])
            state["mmset"] = [0, 1, 2, 3, 4, 5, 6]
            for half, (ukey, dkey, k0, nk) in enumerate((("U0", "D0", 0, 12), ("U1", "D1", 12, 10))):
                P.label = "up"
                for si, sid in enumerate(ids[ukey]):
                    s = use_slab(sid)
                    W = v3(s, 8, 512)
                    for pp_ in range(2):
                        jj = k0 + 2 * si + pp_
                        ja = 2 * si + pp_
                        res = []
                        for part in range(2):
                            bank, bb = mmbank()
                            cols = slice(part * 256 + pp_ * 128, part * 256 + (pp_ + 1) * 128)
                            for kc in range(KC):
                                PE("matmul", r=[bring[s], bxn[kc]], w=[bb], out=bank[:, 0:T + 2], lhsT=W[:, kc, cols],
                                   rhs=xn[:, kc, 0:T + 2], start=(kc == 0), stop=(kc == KC - 1))
                            res.append((bank, bb))
                        for part, (dst, bdst) in enumerate(((cv, bcv), (cg, bcg))):
                            bank, bb = res[part]
                            ch = jj + part * NFF
                            cb = pb_ + 45 + ch
                            cw = lambda tap, ch=ch: ppt[:, pb_ + 89 + tap * 44 + ch:pb_ + 90 + tap * 44 + ch]
                            A("activation", r=[bb, bpp], w=[bdst], out=dst[:, 0:T], in_=bank[:, 2:T + 2],
                              func=AF.Identity, scale=cw(2), bias=ppt[:, cb:cb + 1])
                            V("scalar_tensor_tensor", r=[bb, bpp, bdst], w=[bdst], out=dst[:, 0:T],
                              in0=bank[:, 1:T + 1], scalar=cw(1), in1=dst[:, 0:T], op0=ALU.mult, op1=ALU.add)
                            V("scalar_tensor_tensor", r=[bb, bpp, bdst], w=[bdst], out=dst[:, 0:T],
                              in0=bank[:, 0:T], scalar=cw(0), in1=dst[:, 0:T], op0=ALU.mult, op1=ALU.add)
                        A("activation", r=[bcg], w=[bsg], out=sg[:, 0:T], in_=cg[:, 0:T], func=AF.Silu)
                        V("tensor_tensor", r=[bsg, bcv], w=[baT[ja]], out=aT[:, ja, 0:T], in0=sg[:, 0:T],
                          in1=cv[:, 0:T], op=ALU.mult)
                P.label = "down"
                for qq in (0, 2):
                    slabs = [use_slab(ids[dkey][qq]), use_slab(ids[dkey][qq + 1], first=False)]
                    groups = []
                    for qi in range(2):
                        W = v3(slabs[qi], nk, 256)
                        for mm in range(2):
                            bank, bb = mmbank()
                            groups.append((W, slabs[qi], mm, 2 * (qq + qi) + mm, bank, bb))
                    for (W, s, mm, m, bank, bb) in groups:
                        for k in range(nk - 2):
                            PE("matmul", r=[bring[s], baT[k]], w=[bb], out=bank[:, 0:T],
                               lhsT=W[:, k, mm * 128:(mm + 1) * 128], rhs=aT[:, k, 0:T], start=(k == 0), stop=False)
                    for (W, s, mm, m, bank, bb) in groups:
                        for k in range(nk - 2, nk):
                            PE("matmul", r=[bring[s], baT[k]], w=[bb], out=bank[:, 0:T],
                               lhsT=W[:, k, mm * 128:(mm + 1) * 128], rhs=aT[:, k, 0:T], start=False, stop=(k == nk - 1))
                        V("tensor_tensor", r=[bb, bx[m]], w=[bx[m]], out=x32[:, m, c0:c0 + T], in0=bank[:, 0:T],
                          in1=x32[:, m, c0:c0 + T], op=ALU.add)
                if half == 1 and nxt is not None:
                    emit_norm(nxt[0], nxt[1] * 64, pb_ + 0)
            state["mmset"] = [0, 1, 2]
            if l == 0 and c0 < 128:
                for m in range(KC):
                    V("tensor_scalar", r=[bx[m], bcst], w=[bx[m]], out=x32[:, m, c0:128], in0=x32[:, m, c0:128],
                      scalar1=cstt[:, 1216:1217], scalar2=None, op0=ALU.mult)

        def store_tile(c0, nC, normed):
            T = nC * 64
            lo = max(c0, HALO)
            o = lo - c0
            n = c0 + T - lo
            if n <= 0:
                return
            if normed:
                norm_rstd(c0, T, D)
            for kc in range(KC):
                j = kc % 2
                if normed:
                    gcol = 221 + kc
                    V("scalar_tensor_tensor", r=[bx[kc], bpp, brstd], w=[bostg[j]], out=ostg[j][:, 0:T],
                      in0=x32[:, kc, c0:c0 + T], scalar=ppt[:, gcol:gcol + 1], in1=rstd[:, 0:T],
                      op0=ALU.mult, op1=ALU.mult)
                else:
                    V("tensor_copy", r=[bx[kc]], w=[bostg[j]], out=ostg[j][:, 0:T], in_=x32[:, kc, c0:c0 + T])
                out_toks.append(DMA("sync", "st%d" % j, outT[kc * 128:(kc + 1) * 128, lo - HALO:lo - HALO + n],
                                    ostg[j][:, o:o + n], r=[bostg[j]]))

        for kind, l, tiles, info in plan:
            if kind == "p1":
                pass1(l, tiles, info)
                exchange(l)
            else:
                do_ffn = info
                last = (not do_ffn) or stop_i == 2 * l + 1 or l == L - 1
                for ti, (c0, nC, tids) in enumerate(tiles):
                    mixer_tile(l, ti, c0, nC, tids, have_norm=(do_ffn and ti > 0))
                    if do_ffn:
                        nx = (tiles[ti + 1][0], tiles[ti + 1][1]) if ti + 1 < len(tiles) else None
                        ffn_tile(l, ti, c0, nC, tids, nxt=nx)
                    if last:
                        store_tile(c0, nC, normed=(stop_after == "full"))
                if last:
                    break
        P.wait_all("sync", out_toks)
        for name in ("tensor", "vector", "scalar", "gpsimd", "sync"):
            print("engine", name, "ops", len(P.eng[name].ops), "insts", P.eng[name].n)
        build_program.pe_labels = list(P.pe_labels)
        P.emit_all()
    return nc


def _pack_params(norm1_g, b_gate, pool_scale, lb_logits, hgrn_norm_g, norm2_g, conv_w, conv_b, final_g):
    pp = np.zeros((128, L * NPP), np.float32)
    for l in range(L):
        b = l * NPP
        pp[:, b + 0:b + 8] = norm1_g[l].reshape(8, 128).T
        pp[:, b + 8:b + 24] = b_gate[l].reshape(16, 128).T
        pp[:, b + 24:b + 28] = pool_scale[l].reshape(4, 128).T
        pp[:, b + 28:b + 32] = lb_logits[0].reshape(4, 128).T
        pp[:, b + 32:b + 36] = lb_logits[1].reshape(4, 128).T
        pp[:, b + 36] = hgrn_norm_g[l]
        pp[:, b + 37:b + 45] = norm2_g[l].reshape(8, 128).T
        pp[:, b + 45:b + 89] = conv_b[l].reshape(44, 128).T
        for tap in range(3):
            pp[:, b + 89 + tap * 44:b + 89 + (tap + 1) * 44] = conv_w[l, tap].reshape(44, 128).T
        pp[:, b + 221:b + 229] = final_g.reshape(8, 128).T
    return pp


def _consts(j, b):
    c = np.zeros((128, NCST), np.float32)
    rm = np.ones(512, np.float32)
    rm[0::64] = 0.0
    c[:, 0:512] = rm[None, :]
    s = np.arange(64)[:, None]
    t = np.arange(64)[None, :]
    cm = (s <= t).astype(np.float32)
    c[0:64, 512:1024] = np.tile(cm, (1, 8))
    c[:, 1024:1152] = np.eye(128, dtype=np.float32)
    for g in range(4):
        w = 2 << g
        if j == 0:
            cnt = np.minimum(np.arange(1, 17), w).astype(np.float32)
        else:
            cnt = np.full(16, w, np.float32)
        c[:, 1152 + g * 16:1168 + g * 16] = (1.0 / cnt)[None, :]
    c[:, 1216] = 0.0 if j == 0 else 1.0
    me = b * 4 + j
    for r in range(NCORE):
        a = 1.0 if (r // 4 == b and r < me) else 0.0
        c[:, 1217 + r] = a
        c[:, 1225 + r] = 1.0 - a
    return c


_PROG_CACHE = {}


def _run(inputs, stop_after="full", debug=False):
    x = np.asarray(inputs["x"], np.float32)
    f = lambda k: np.ascontiguousarray(np.asarray(inputs[k], np.float32))
    pp = _pack_params(f("norm1_g"), f("b_gate"), f("pool_scale"), f("lb_logits"), f("hgrn_norm_g"),
                      f("norm2_g"), f("conv_w"), f("conv_b"), f("final_g"))
    shared = {k: f(k) for k in ("w_in", "pool_w", "w_pa", "w_pb", "w_o", "w_up", "w_down")}
    in_maps = []
    for c in range(NCORE):
        b, j = divmod(c, 4)
        s = j * OWN
        xe = np.zeros((EXT, D), np.float32)
        lo = s - HALO
        if lo < 0:
            xe[-lo:] = x[b, 0:s + OWN]
        else:
            xe[:] = x[b, lo:s + OWN]
        m = {"xT": np.ascontiguousarray(xe.T), "pp": pp, "cst": _consts(j, b)}
        m.update(shared)
        in_maps.append(m)
    key = (stop_after, debug)
    if key not in _PROG_CACHE:
        _PROG_CACHE[key] = build_program(stop_after, debug)
    nc = _PROG_CACHE[key]
    res = run_bass_kernel_spmd(nc, in_maps, core_ids=list(range(NCORE)))
    out = np.empty((NB, SEQ, D), np.float32)
    for c in range(NCORE):
        b, j = divmod(c, 4)
        out[b, j * OWN:(j + 1) * OWN, :] = res.results[c]["outT"].T
    if debug:
        return out, [res.results[c]["dbg"] for c in range(NCORE)], list(build_program.dbg_names)
    return out


def kernel(**inputs):
    return _run(inputs, "full")
```

```python
import os
import numpy as np
from contextlib import ExitStack
import concourse.bass as bass
import concourse.mybir as mybir
from concourse.bass_utils import run_bass_kernel_spmd

F32 = mybir.dt.float32
BF16 = mybir.dt.bfloat16
AF = mybir.ActivationFunctionType
ALU = mybir.AluOpType

D = 1024
SEQ = 8192
NB = 2
L = 2
NCORE = 8
OWN = 2048
HALO = 128
EXT = OWN + HALO
KC = 8
DFF = 2816
NFF = 22
INC = 4608
EPS = 1e-6
TMAX = 448
NPP = 232
NCST = 1240
NSLOT = 4
REUSE_BF16 = False
EXACT_L1 = True
SLOTB = 4096


class Buf:
    __slots__ = ("name", "w", "r")

    def __init__(self, name):
        self.name = name
        self.w = None
        self.r = []


class Eng:
    def __init__(self, name, sem):
        self.name = name
        self.sem = sem
        self.n = 0
        self.ops = []
        self.waited = {}


class Prog:
    def __init__(self, nc, stack):
        self.nc = nc
        self.stack = stack
        self.sems = {}
        self.eng = {}
        for name in ("tensor", "vector", "scalar", "gpsimd", "sync"):
            s = stack.enter_context(nc.semaphore("s_" + name))
            self.sems["e_" + name] = s
            self.eng[name] = Eng(name, "e_" + name)
        self.dma_cnt = {}
        self.nbuf = 0
        self.label = ""
        self.pe_labels = []

    def buf(self, name=None):
        self.nbuf += 1
        return Buf(name or ("b%d" % self.nbuf))

    def bufs(self, n, name="b"):
        return [self.buf("%s%d" % (name, i)) for i in range(n)]

    def dma_sem(self, name):
        key = "d_" + name
        if key not in self.sems:
            self.sems[key] = self.stack.enter_context(self.nc.semaphore(key))
            self.dma_cnt[key] = 0
        return key

    def _need(self, eng, tok, same_engine_ok=False):
        if tok is None:
            return
        key, val = tok
        if same_engine_ok and key == eng.sem:
            return
        if eng.waited.get(key, 0) >= val:
            return
        eng.waited[key] = val
        sem = self.sems[key]
        eng.ops.append(lambda e, sem=sem, val=val: e.wait_ge(sem, val))

    def _deps(self, eng, reads, writes):
        pe = eng.name == "tensor"
        for b in reads:
            self._need(eng, b.w, same_engine_ok=pe)
        for b in writes:
            self._need(eng, b.w, same_engine_ok=pe)
            for t in b.r:
                self._need(eng, t, same_engine_ok=True)

    @staticmethod
    def _compact(toks):
        best = {}
        for k, v in toks:
            if best.get(k, 0) < v:
                best[k] = v
        return list(best.items())

    def _commit(self, tok, reads, writes):
        for b in reads:
            b.r.append(tok)
            if len(b.r) > 48:
                b.r = self._compact(b.r)
        for b in writes:
            b.w = tok
            b.r = []

    def op(self, engname, emit, reads=(), writes=()):
        eng = self.eng[engname]
        self._deps(eng, reads, writes)
        eng.n += 1
        if engname == "tensor":
            self.pe_labels.append(self.label)
        tok = (eng.sem, eng.n)
        sem = self.sems[eng.sem]
        eng.ops.append(lambda e, emit=emit, sem=sem: emit(e).then_inc(sem, 1))
        self._commit(tok, reads, writes)
        return tok

    def dma(self, qname, semname, emit, reads=(), writes=(), inc=16):
        eng = self.eng[qname]
        self._deps(eng, reads, writes)
        key = self.dma_sem(semname)
        self.dma_cnt[key] += inc
        tok = (key, self.dma_cnt[key])
        sem = self.sems[key]
        eng.ops.append(lambda e, emit=emit, sem=sem, inc=inc: emit(e).then_inc(sem, inc))
        self._commit(tok, reads, writes)
        return tok

    def dma_group(self, qname, semname, emits, reads=(), writes=()):
        eng = self.eng[qname]
        self._deps(eng, reads, writes)
        key = self.dma_sem(semname)
        sem = self.sems[key]
        for emit in emits:
            self.dma_cnt[key] += 16
            eng.ops.append(lambda e, emit=emit, sem=sem: emit(e).then_inc(sem, 16))
        tok = (key, self.dma_cnt[key])
        self._commit(tok, reads, writes)
        return tok

    def wait_all(self, engname, toks):
        eng = self.eng[engname]
        for t in toks:
            self._need(eng, t)

    def emit_all(self):
        nc = self.nc
        with nc.Block() as block:
            for name, dec in (("sync", block.sync), ("gpsimd", block.gpsimd),
                              ("scalar", block.scalar), ("vector", block.vector),
                              ("tensor", block.tensor)):
                ops = self.eng[name].ops
                if not ops:
                    continue

                def body(e, ops=ops):
                    for f in ops:
                        f(e)
                dec(body)


def chunk_tiles(c0, nchunks, per=7):
    nt = (nchunks + per - 1) // per
    base, rem = divmod(nchunks, nt)
    out = []
    col = c0
    for i in range(nt):
        n = base + (1 if i < rem else 0)
        out.append((col, n))
        col += n * 64
    return out


def build_program(stop_after="full", debug=False):
    nc = bass.Bass("TRN2", target_bir_lowering=False)
    dbg = nc.dram_tensor("dbg", [128, 16 * 512], F32, kind="ExternalOutput").ap() if debug else None
    dram = lambda name, shape, kind: nc.dram_tensor(name, shape, F32, kind=kind).ap()
    xT = dram("xT", [D, EXT], "ExternalInput")
    pp = dram("pp", [128, L * NPP], "ExternalInput")
    cst = dram("cst", [128, NCST], "ExternalInput")
    w_in = dram("w_in", [L, D, INC], "ExternalInput")
    pool_w = dram("pool_w", [L, 4, 128, 128], "ExternalInput")
    w_pa = dram("w_pa", [L, 512, D], "ExternalInput")
    w_pb = dram("w_pb", [L, 512, D], "ExternalInput")
    w_o = dram("w_o", [L, D, D], "ExternalInput")
    w_up = dram("w_up", [L, D, 2 * DFF], "ExternalInput")
    w_down = dram("w_down", [L, DFF, D], "ExternalInput")
    outT = dram("outT", [D, OWN], "ExternalOutput")
    cc_in = [nc.dram_tensor("cc_in%d" % l, [128, 516], F32) for l in range(L)]
    cc_out = [nc.dram_tensor("cc_out%d" % l, [NCORE * 128, 516], F32) for l in range(L)]

    with ExitStack() as st:
        P = Prog(nc, st)
        sb = lambda name, shape, dt: st.enter_context(nc.sbuf_tensor(name, shape, dt))

        def mk(engname):
            def f(name, r=(), w=(), **kw):
                return P.op(engname, lambda e, name=name, kw=kw: getattr(e, name)(**kw), reads=r, writes=w)
            return f
        V, A, G, PE = mk("vector"), mk("scalar"), mk("gpsimd"), mk("tensor")

        def DMA(q, semname, dst, src, r=(), w=(), inc=16):
            return P.dma(q, semname, lambda e, dst=dst, src=src: e.dma_start(out=dst, in_=src),
                         reads=r, writes=w, inc=inc)

        x32 = sb("x32", [128, KC, EXT], F32); bx = P.bufs(KC, "x")
        ring = [sb("ring%d" % i, [128, SLOTB], BF16) for i in range(NSLOT)]
        bring = P.bufs(NSLOT, "ring")
        ppt = sb("ppt", [128, L * NPP], F32); bpp = P.buf("pp")
        cstt = sb("cstt", [128, NCST], F32); bcst = P.buf("cst")
        identb = sb("identb", [128, 128], BF16); bident = P.buf("ident")
        onesb = sb("onesb", [128, 128], BF16); bones = P.buf("ones")
        mhalf = sb("mhalf", [128, 2], F32); bmhalf = P.buf("mhalf")
        hbg = sb("hbg", [128, L * 16], F32); bhbg = P.buf("hbg")
        lbc = sb("lbc", [128, 6, L * 4], F32); blbc = P.buf("lbc")
        pwb = sb("pwb", [128, L * 4, 128], BF16); bpwb = P.buf("pwb")
        xn = sb("xn", [128, KC, 2 + TMAX], BF16); bxn = P.bufs(KC, "xn")
        xcar = sb("xcar", [128, KC, 2], BF16); bxcar = P.buf("xcar")
        sqb = sb("sqb", [128, 2, TMAX], BF16); bsqb = P.bufs(2, "sqb")
        rs1 = sb("rs1", [128, TMAX], F32); brs1 = P.buf("rs1")
        rstd = sb("rstd", [128, TMAX], F32); brstd = P.buf("rstd")
        sq = sb("sq", [128, 4, TMAX], BF16); bsq = P.bufs(4, "sq")
        szo = sb("szo", [128, 4, TMAX], BF16); bszo = P.bufs(4, "szo")
        thn = sb("thn", [128, 4, TMAX], F32); bthn = P.bufs(4, "thn")
        vt = sb("vt", [64, 4, 7 * 128], BF16); bvt = P.bufs(4, "vt")
        gG = sb("gG", [128, TMAX], F32); bgG = P.buf("gG")
        gB = sb("gB", [128, TMAX], F32); bgB = P.buf("gB")
        gE = sb("gE", [128, TMAX], F32); bgE = P.buf("gE")
        ebl = sb("ebl", [128, 8], F32); bebl = P.buf("ebl")
        qh = sb("qh", [128, TMAX], BF16); bqh = P.buf("qh")
        kt = sb("kt", [128, TMAX], BF16); bkt = P.buf("kt")
        kh = sb("kh", [128, TMAX], BF16); bkh = P.buf("kh")
        khtok = sb("khtok", [64, 7 * 128], BF16); bkhtok = P.buf("khtok")
        attnm = sb("attnm", [64, TMAX], BF16); battnm = P.buf("attnm")
        S32 = sb("S32", [128, 4, 128], F32); bS32 = P.bufs(4, "S32")
        Sbf = sb("Sbf", [128, 7, 128], BF16); bSbf = P.bufs(7, "Sbf")
        bsum = sb("bsum", [128, 8], F32); bbsum = P.buf("bsum")
        osq = sb("osq", [128, TMAX], BF16); bosq = P.buf("osq")
        onf = sb("onf", [128, TMAX], F32); bonf = P.buf("onf")
        onb = sb("onb", [128, 4, TMAX], BF16); bonb = P.bufs(4, "onb")
        ccs = sb("ccs", [128, 516], F32); bccs = P.buf("ccs")
        ccr = sb("ccr", [128, 516], F32); bccr = P.buf("ccr")
        agt = sb("agt", [128, 520], F32); bagt = P.buf("agt")
        pcar = sb("pcar", [128, 4, 16], F32); bpcar = P.buf("pcar")
        ut = sb("ut", [128, 1, 16 + TMAX], F32); but = P.bufs(1, "ut")
        psA = sb("psA", [128, 16 + TMAX], F32); bpsA = P.buf("psA")
        psB = sb("psB", [128, 16 + TMAX], F32); bpsB = P.buf("psB")
        pooled = sb("pooled", [128, 4, TMAX], BF16); bpooled = P.bufs(4, "pooled")
        mixed = sb("mixed", [128, 4, TMAX], BF16); bmixed = P.bufs(4, "mixed")
        tht = sb("tht", [128, 2, TMAX], F32); btht = P.bufs(2, "tht")
        At = sb("At", [128, TMAX], F32); bAt = P.buf("At")
        Bt = sb("Bt", [128, TMAX], F32); bBt = P.buf("Bt")
        Mb = sb("Mb", [128, KC, TMAX], BF16); bMb = P.bufs(KC, "Mb")
        cv, bcv = gG, bgG
        cg, bcg = gB, bgB
        sg, bsg = gE, bgE
        aT = sb("aT", [128, 12, TMAX], BF16); baT = P.bufs(12, "aT")
        ostg = [At, Bt]; bostg = [bAt, bBt]
        print("SBUF bytes remaining per partition:", nc.sbuf_bytes_remaining)

        dbgst = sb("dbgst", [128, 512], F32) if debug else None
        bdbg = P.buf("dbg")
        dbg_names = []

        def dump(name, src, rbufs, ncol, parts=128):
            if not debug or len(dbg_names) >= 16:
                return
            i = len(dbg_names)
            dbg_names.append(name)
            V("memset", w=[bdbg], ap=dbgst[:], constant=0.0)
            V("tensor_copy", r=list(rbufs), w=[bdbg], out=dbgst[0:parts, 0:ncol], in_=src)
            out_toks.append(DMA("sync", "dbgst", dbg[:, i * 512:(i + 1) * 512], dbgst[:], r=[bdbg]))
        build_program.dbg_names = dbg_names

        pbank = [st.enter_context(nc.psum_tensor("pb%d" % i, [128, 512], F32)) for i in range(7)]
        bpb = P.bufs(7, "pb")
        ptr = st.enter_context(nc.psum_tensor("ptr", [128, 1024], BF16)); bptr = P.buf("ptr")
        state = {"mm": 0, "mmset": [0, 1, 2]}

        def mmbank():
            s = state["mmset"]
            i = s[state["mm"] % len(s)]
            state["mm"] += 1
            return pbank[i], bpb[i]

        seq = []

        def v3(slot, k, n):
            return ring[slot][:, 0:k * n].rearrange("p (k n) -> p k n", k=k)

        def src_kn(ap2d, r0, nk, c0, ncol):
            return ap2d[r0:r0 + nk * 128, c0:c0 + ncol].rearrange("(k p) n -> p k n", p=128)

        issued = {"n": 0}

        wscr = nc.dram_tensor("wscr", [80, 128, SLOTB], BF16)
        keyslot = {}
        bscr = {}

        def live(i):
            while issued["n"] < len(seq) and issued["n"] <= i + NSLOT - 1:
                j = issued["n"]
                slot = j % NSLOT
                key, nelem, parts = seq[j]
                if key is None or key not in keyslot:
                    emits = [(lambda e, dst=dst_fn(slot), src=src: e.dma_start(out=dst, in_=src))
                             for dst_fn, src in parts]
                    P.dma_group("gpsimd", "ring%d" % slot, emits, writes=[bring[slot]])
                    if key is not None and REUSE_BF16:
                        k = len(keyslot)
                        keyslot[key] = k
                        bscr[key] = P.buf("scr%d" % k)
                        DMA("sync", "scrst%d" % (k % 4), wscr.ap()[k, :, 0:nelem], ring[slot][:, 0:nelem],
                            r=[bring[slot]], w=[bscr[key]])
                else:
                    k = keyslot[key]
                    DMA("sync", "ringl%d" % slot, ring[slot][:, 0:nelem], wscr.ap()[k, :, 0:nelem],
                        r=[bscr[key]], w=[bring[slot]])
                issued["n"] += 1

        def use_slab(i, first=True, min_live=None):
            if min_live is not None:
                live(min_live)
            elif first:
                live(i)
            assert issued["n"] > i, (issued["n"], i)
            return i % NSLOT

        def add_slab(parts, nelem=SLOTB, key=None):
            seq.append((key, nelem, parts))
            return len(seq) - 1

        def slab_H(l, h):
            parts = []
            for j, base in enumerate((512, 1024, 2048, 1536)):
                parts.append((lambda s, j=j: v3(s, 8, 512)[:, :, j * 128:(j + 1) * 128],
                              src_kn(w_in[l], 0, 8, base + h * 128, 128)))
            return add_slab(parts, 4096, ("H", l, h))

        def slab_full(ap2d, c0, key=None):
            return add_slab([(lambda s: v3(s, 8, 512), src_kn(ap2d, 0, 8, c0, 512))], 4096, key)

        def slab_pair(ap2d, ca, cb, key=None):
            return add_slab([(lambda s: v3(s, 8, 512)[:, :, 0:256], src_kn(ap2d, 0, 8, ca, 256)),
                             (lambda s: v3(s, 8, 512)[:, :, 256:512], src_kn(ap2d, 0, 8, cb, 256))], 4096, key)

        def slab_P(l, i):
            return add_slab([(lambda s: v3(s, 8, 256)[:, 0:4, :], src_kn(w_pa[l], 0, 4, i * 256, 256)),
                             (lambda s: v3(s, 8, 256)[:, 4:8, :], src_kn(w_pb[l], 0, 4, i * 256, 256))],
                            2048, ("P", l, i))

        def slab_D(l, k0, nk, q):
            return add_slab([(lambda s, nk=nk: v3(s, nk, 256), src_kn(w_down[l], k0 * 128, nk, q * 256, 256))],
                            nk * 256, ("D", l, k0, q))

        out_toks = []
        stages = ["l0mix", "l0ffn", "l1mix", "l1ffn", "full"]
        stop_i = stages.index(stop_after)

        plan = []
        for l in range(L):
            if 2 * l > stop_i:
                break
            E0 = 0
            p1 = chunk_tiles(E0, OWN // 64 + (1 if (l == 1 and EXACT_L1) else 0))
            p2 = chunk_tiles(E0, (EXT - E0) // 64)
            ids1 = {"F": slab_full(w_in[l], 1024), "I": slab_full(w_in[l], 1536)}
            plan.append(("p1", l, p1, ids1))
            do_ffn = stop_i >= 2 * l + 1
            tiles2 = []
            for ti, (c0, nC) in enumerate(p2):
                ids = {}
                ids["QFO"] = [slab_full(w_in[l], 512, ("Q", l)), slab_full(w_in[l], 1024, ("F", l)),
                              slab_full(w_in[l], 2048, ("O", l))]
                ids["I"] = slab_full(w_in[l], 1536, ("I", l))
                ids["Wp"] = slab_full(w_in[l], 0, ("Wp", l))
                ids["GP"] = []
                for i in range(4):
                    g = slab_pair(w_in[l], 2560 + i * 256, 3584 + i * 256, ("G", l, i))
                    p = slab_P(l, i)
                    ids["GP"].append((g, p))
                ids["O"] = [slab_full(w_o[l], i * 512, ("O", l, i)) for i in range(2)]
                if do_ffn:
                    ids["U0"] = [slab_pair(w_up[l], s * 256, DFF + s * 256, ("U", l, s)) for s in range(6)]
                    ids["D0"] = [slab_D(l, 0, 12, q) for q in range(4)]
                    ids["U1"] = [slab_pair(w_up[l], s * 256, DFF + s * 256, ("U", l, s)) for s in range(6, 11)]
                    ids["D1"] = [slab_D(l, 12, 10, q) for q in range(4)]
                tiles2.append((c0, nC, ids))
            plan.append(("p2", l, tiles2, do_ffn))

        for kc in range(KC):
            DMA("sync", "ldx%d" % kc, x32[:, kc, :], xT[kc * 128:(kc + 1) * 128, :], w=[bx[kc]])
        DMA("sync", "ldpp", ppt[:], pp, w=[bpp])
        DMA("sync", "ldcst", cstt[:], cst, w=[bcst])
        DMA("gpsimd", "ldpw", pwb[:], pool_w.rearrange("l g c d -> c (l g) d"), w=[bpwb])
        V("tensor_copy", r=[bcst], w=[bident], out=identb[:], in_=cstt[:, 1024:1152])
        V("memset", w=[bones], ap=onesb[:], constant=1.0)
        V("memset", w=[bmhalf], ap=mhalf[:], constant=EPS)
        V("memset", w=[bxcar], ap=xcar[:], constant=0.0)
        for t_, b_ in ((psA, bpsA), (psB, bpsB), (gG, bgG), (gB, bgB), (gE, bgE), (rs1, brs1), (rstd, brstd),
                       (At, bAt), (Bt, bBt), (onf, bonf), (agt, bagt), (ccr, bccr), (ccs, bccs), (ebl, bebl),
                       (bsum, bbsum)):
            V("memset", w=[b_], ap=t_[:], constant=0.0)
        for t_, bl_ in ((ut, but), (tht, btht)):
            G("memset", w=list(bl_), ap=t_[:], constant=0.0)
        for t_, b_ in ((kt, bkt), (kh, bkh), (qh, bqh), (osq, bosq), (attnm, battnm), (khtok, bkhtok)):
            G("memset", w=[b_], ap=t_[:], constant=0.0)
        for t_, bl_ in ((sqb, bsqb), (xn, bxn), (sq, bsq), (szo, bszo), (thn, bthn), (vt, bvt), (onb, bonb),
                        (pooled, bpooled), (mixed, bmixed), (Mb, bMb), (aT, baT), (Sbf, bSbf)):
            G("memset", w=list(bl_), ap=t_[:], constant=0.0)
        for l in range(L):
            V("tensor_scalar", r=[bpp], w=[bhbg], out=hbg[:, l * 16:(l + 1) * 16],
              in0=ppt[:, l * NPP + 8:l * NPP + 24], scalar1=0.5, scalar2=None, op0=ALU.mult)
        V("memset", w=[blbc], ap=lbc[:], constant=0.0)
        V("tensor_tensor", r=[bpp, blbc], w=[blbc], out=lbc[:, 4, 0:4], in0=ppt[:, 32:36], in1=ppt[:, 28:32],
          op=ALU.subtract)
        A("activation", r=[blbc], w=[blbc], out=lbc[:, 5, 0:4], in_=lbc[:, 4, 0:4], func=AF.Tanh, scale=0.5)
        V("tensor_scalar", r=[blbc], w=[blbc], out=lbc[:, 0, 4:8], in0=lbc[:, 5, 0:4], scalar1=0.5, scalar2=0.5,
          op0=ALU.mult, op1=ALU.add)
        V("tensor_scalar", r=[blbc], w=[blbc], out=lbc[:, 1, :], in0=lbc[:, 0, :], scalar1=-0.5, scalar2=0.5,
          op0=ALU.mult, op1=ALU.add)
        V("tensor_scalar", r=[blbc], w=[blbc], out=lbc[:, 2, :], in0=lbc[:, 0, :], scalar1=0.5, scalar2=0.5,
          op0=ALU.mult, op1=ALU.add)
        V("tensor_scalar", r=[blbc], w=[blbc], out=lbc[:, 3, :], in0=lbc[:, 0, :], scalar1=0.5, scalar2=-0.5,
          op0=ALU.mult, op1=ALU.add)

        def norm_rstd(c0, T, dim):
            P.label = "norm"
            bank, bb = mmbank()
            for kc in range(KC):
                j = kc % 2
                A("activation", r=[bx[kc]], w=[bsqb[j]], out=sqb[:, j, 0:T], in_=x32[:, kc, c0:c0 + T],
                  func=AF.Square)
                PE("matmul", r=[bsqb[j], bones], w=[bb], out=bank[:, 0:T], lhsT=onesb[:], rhs=sqb[:, j, 0:T],
                   start=(kc == 0), stop=(kc == KC - 1))
            A("activation", r=[bb, bmhalf], w=[brs1], out=rs1[:, 0:T], in_=bank[:, 0:T], func=AF.Ln,
              scale=1.0 / dim, bias=mhalf[:, 0:1])
            A("activation", r=[brs1], w=[brstd], out=rstd[:, 0:T], in_=rs1[:, 0:T], func=AF.Exp, scale=-0.5)

        def emit_norm(c0, T, gbase):
            norm_rstd(c0, T, D)
            for kc in range(KC):
                V("scalar_tensor_tensor", r=[bx[kc], bpp, brstd], w=[bxn[kc]], out=xn[:, kc, 2:2 + T],
                  in0=x32[:, kc, c0:c0 + T], scalar=ppt[:, gbase + kc:gbase + kc + 1], in1=rstd[:, 0:T],
                  op0=ALU.mult, op1=ALU.mult)

        def proj_chunk(slot, colsl, T, xoff=2):
            bank, bb = mmbank()
            W = v3(slot, 8, 512)
            for kc in range(KC):
                PE("matmul", r=[bring[slot], bxn[kc]], w=[bb], out=bank[:, 0:T], lhsT=W[:, kc, colsl],
                   rhs=xn[:, kc, xoff:xoff + T], start=(kc == 0), stop=(kc == KC - 1))
            return bank, bb

        def proj_v(slot, colsl, nC, h):
            W = v3(slot, 8, 512)
            for c_lo in range(0, nC, 4):
                bank, bb = mmbank()
                n = min(4, nC - c_lo)
                for cc in range(n):
                    c = c_lo + cc
                    for kc in range(KC):
                        PE("matmul", r=[bring[slot], bxn[kc]], w=[bb], out=bank[0:64, cc * 128:(cc + 1) * 128],
                           lhsT=xn[:, kc, 2 + c * 64:2 + (c + 1) * 64], rhs=W[:, kc, colsl],
                           start=(kc == 0), stop=(kc == KC - 1))
                A("activation", r=[bb], w=[bvt[h]], out=vt[:, h, c_lo * 128:(c_lo + n) * 128],
                  in_=bank[0:64, 0:n * 128], func=AF.Identity)

        ebl2 = sb("ebl2", [128, 8], F32); bebl2 = P.buf("ebl2")
        V("memset", w=[bebl2], ap=ebl2[:], constant=0.0)
        HS = [dict(gG=gG, bgG=bgG, gB=gB, bgB=bgB, gE=gE, bgE=bgE, kt=kt, bkt=bkt, kh=kh, bkh=bkh, qh=qh, bqh=bqh,
                   ebl=ebl, bebl=bebl),
              dict(gG=tht[:, 0, :], bgG=btht[0], gB=tht[:, 1, :], bgB=btht[1], gE=At, bgE=bAt,
                   kt=Mb[:, 0, :], bkt=bMb[0], kh=Mb[:, 1, :], bkh=bMb[1], qh=Mb[:, 2, :], bqh=bMb[2],
                   ebl=ebl2, bebl=bebl2)]

        SbfB = Mb[:, 5:7, :].rearrange("p a t -> p (a t)").rearrange("p (c v) -> p c v", v=128)
        SBF = [(lambda c: Sbf[:, c, :], lambda c: [bSbf[c]]),
               (lambda c: SbfB[:, c, :], lambda c: [bMb[5], bMb[6]])]

        def hgrn_s1(l, h, nC, full, neutral0, S):
            P.label = "hstate%d" % (2 if full else 1)
            T = nC * 64
            li = l * 4 + h
            gG_, gB_, gE_, kt_, kh_, qh_, ebl_ = S["gG"], S["gB"], S["gE"], S["kt"], S["kh"], S["qh"], S["ebl"]
            A("activation", r=[bthn[h], blbc], w=[S["bgG"]], out=gG_[:, 0:T], in_=thn[:, h, 0:T], func=AF.Ln,
              scale=lbc[:, 3, li:li + 1], bias=lbc[:, 2, li:li + 1])
            if neutral0 and EXACT_L1:
                V("memset", w=[S["bgG"]], ap=gG_[:, 0:64], constant=0.0)
            V("tensor_tensor_scan", r=[S["bgG"], bcst], w=[S["bgB"]], out=gB_[:, 0:T], data0=cstt[:, 0:T],
              data1=gG_[:, 0:T], initial=0.0, op0=ALU.mult, op1=ALU.add)
            A("activation", r=[S["bgB"]], w=[S["bgE"]], out=gE_[:, 0:T], in_=gB_[:, 0:T], func=AF.Exp, scale=-1.0)
            A("activation", r=[S["bgB"]], w=[S["bebl"]], out=ebl_[:, 0:nC], in_=gB_[:, 63:T:64], func=AF.Exp)
            V("scalar_tensor_tensor", r=[bthn[h], S["bgE"]], w=[S["bkt"]], out=kt_[:, 0:T], in0=thn[:, h, 0:T],
              scalar=1.0, in1=gE_[:, 0:T], op0=ALU.add, op1=ALU.mult)
            V("tensor_tensor", r=[S["bkt"], S["bebl"]], w=[S["bkh"]],
              out=kh_[:, 0:T].rearrange("p (c t) -> p c t", t=64),
              in0=kt_[:, 0:T].rearrange("p (c t) -> p c t", t=64),
              in1=ebl_[:, 0:nC].unsqueeze(2).to_broadcast([128, nC, 64]), op=ALU.mult)
            if full:
                A("activation", r=[S["bgB"]], w=[S["bgG"]], out=gG_[:, 0:T], in_=gB_[:, 0:T], func=AF.Exp)
                V("scalar_tensor_tensor", r=[bsq[h], blbc, S["bgG"]], w=[S["bqh"]], out=qh_[:, 0:T],
                  in0=sq[:, h, 0:T], scalar=lbc[:, 1, li:li + 1], in1=gG_[:, 0:T], op0=ALU.mult, op1=ALU.mult)
            else:
                V("tensor_reduce", r=[S["bgB"]], w=[bbsum], out=bsum[:, 4 + h:5 + h], in_=gB_[:, 63:T:64],
                  axis=mybir.AxisListType.X, op=ALU.add)
                V("tensor_tensor", r=[bbsum], w=[bbsum], out=bsum[:, h:h + 1], in0=bsum[:, h:h + 1],
                  in1=bsum[:, 4 + h:5 + h], op=ALU.add)

        def hgrn_s2(l, h, nC, full, S, neutral0=False):
            P.label = "hstate%d" % (2 if full else 1)
            kh_, ebl_ = S["kh"], S["ebl"]
            if neutral0 and EXACT_L1:
                V("memset", w=[bvt[h]], ap=vt[:, h, 0:128], constant=0.0)
            for c in range(nC):
                PE("transpose", r=[S["bkh"], bident], w=[bptr], out=ptr[0:64, c * 128:(c + 1) * 128],
                   in_=kh_[:, c * 64:(c + 1) * 64], identity=identb[:])
            A("activation", r=[bptr], w=[bkhtok], out=khtok[:, 0:nC * 128], in_=ptr[0:64, 0:nC * 128],
              func=AF.Identity)
            for c in range(nC):
                bi = 4 + c // 4
                PE("matmul", r=[bkhtok, bvt[h]], w=[bpb[bi]], out=pbank[bi][:, (c % 4) * 128:(c % 4 + 1) * 128],
                   lhsT=khtok[:, c * 128:(c + 1) * 128], rhs=vt[:, h, c * 128:(c + 1) * 128], start=True, stop=True)
            for c in range(nC):
                bi = 4 + c // 4
                if full:
                    V("tensor_copy", r=[bS32[h]], w=SBF[h % 2][1](c), out=SBF[h % 2][0](c), in_=S32[:, h, :])
                V("scalar_tensor_tensor", r=[bS32[h], S["bebl"], bpb[bi]], w=[bS32[h]], out=S32[:, h, :],
                  in0=S32[:, h, :], scalar=ebl_[:, c:c + 1], in1=pbank[bi][:, (c % 4) * 128:(c % 4 + 1) * 128],
                  op0=ALU.mult, op1=ALU.add)

        def hgrn_out_a(l, h, nC, S):
            P.label = "hout"
            T = nC * 64
            kt, bkt, qh, bqh = S["kt"], S["bkt"], S["qh"], S["bqh"]
            pat, bpat = pbank[3], bpb[3]
            for c in range(nC):
                PE("matmul", r=[bkt, bqh], w=[bpat], out=pat[0:64, c * 64:(c + 1) * 64],
                   lhsT=kt[:, c * 64:(c + 1) * 64], rhs=qh[:, c * 64:(c + 1) * 64], start=True, stop=True)
            V("tensor_tensor", r=[bpat, bcst], w=[battnm], out=attnm[:, 0:T], in0=pat[0:64, 0:T],
              in1=cstt[0:64, 512:512 + T], op=ALU.mult)

        def hgrn_out_b(l, h, nC, S):
            P.label = "hout"
            T = nC * 64
            qh, bqh = S["qh"], S["bqh"]
            pat, bpat = pbank[3], bpb[3]
            po, bpo = pbank[6], bpb[6]
            for c in range(nC):
                PE("matmul", r=[bvt[h], battnm], w=[bpo], out=po[:, c * 64:(c + 1) * 64],
                   lhsT=vt[:, h, c * 128:(c + 1) * 128], rhs=attnm[:, c * 64:(c + 1) * 64], start=True, stop=False)
                PE("matmul", r=SBF[h % 2][1](c) + [bqh], w=[bpo], out=po[:, c * 64:(c + 1) * 64],
                   lhsT=SBF[h % 2][0](c), rhs=qh[:, c * 64:(c + 1) * 64], start=False, stop=True)
            A("activation", r=[bpo], w=[bosq], out=osq[:, 0:T], in_=po[:, 0:T], func=AF.Square)
            PE("matmul", r=[bosq, bones], w=[bpat], out=pat[:, 0:T], lhsT=onesb[:], rhs=osq[:, 0:T],
               start=True, stop=True)
            A("activation", r=[bpat, bmhalf], w=[brs1], out=rs1[:, 0:T], in_=pat[:, 0:T], func=AF.Ln,
              scale=1.0 / 128, bias=mhalf[:, 0:1])
            A("activation", r=[brs1], w=[brstd], out=rstd[:, 0:T], in_=rs1[:, 0:T], func=AF.Exp, scale=-0.5)
            gcol = l * NPP + 36
            V("scalar_tensor_tensor", r=[bpo, bpp, brstd], w=[bonf], out=onf[:, 0:T], in0=po[:, 0:T],
              scalar=ppt[:, gcol:gcol + 1], in1=rstd[:, 0:T], op0=ALU.mult, op1=ALU.mult)
            V("tensor_tensor", r=[bonf, bszo[h]], w=[bonb[h]], out=onb[:, h, 0:T], in0=onf[:, 0:T],
              in1=szo[:, h, 0:T], op=ALU.mult)

        def pass1_tile(l, c0, nC, ids):
            T = nC * 64
            emit_norm(c0, T, l * NPP + 0)
            P.label = "p1proj"
            sF = use_slab(ids["F"])
            sI = use_slab(ids["I"], first=False)
            for h in range(4):
                bank, bb = proj_chunk(sF, slice(h * 128, (h + 1) * 128), T)
                A("activation", r=[bb], w=[bthn[h]], out=thn[:, h, 0:T], in_=bank[:, 0:T], func=AF.Tanh, scale=-0.5)
            n0 = (l == 1 and c0 == 0)

            def vproj(h):
                P.label = "p1proj"
                proj_v(sI, slice(h * 128, (h + 1) * 128), nC, h)
            hgrn_s1(l, 0, nC, False, n0, HS[0])
            hgrn_s1(l, 1, nC, False, n0, HS[1])
            vproj(0)
            vproj(1)
            hgrn_s2(l, 0, nC, False, HS[0], n0)
            vproj(2)
            hgrn_s1(l, 2, nC, False, n0, HS[0])
            hgrn_s2(l, 1, nC, False, HS[1], n0)
            vproj(3)
            hgrn_s1(l, 3, nC, False, n0, HS[1])
            hgrn_s2(l, 2, nC, False, HS[0], n0)
            hgrn_s2(l, 3, nC, False, HS[1], n0)

        def pass1(l, tiles, ids):
            for h in range(4):
                V("memset", w=[bS32[h]], ap=S32[:, h, :], constant=0.0)
            V("memset", w=[bbsum], ap=bsum[:], constant=0.0)
            for (c0, nC) in tiles:
                pass1_tile(l, c0, nC, ids)
            for h in range(4):
                V("tensor_copy", r=[bS32[h]], w=[bccs], out=ccs[:, h * 128:(h + 1) * 128], in_=S32[:, h, :])
            A("activation", r=[bbsum], w=[bccs], out=ccs[:, 512:516], in_=bsum[:, 0:4], func=AF.Exp)

        def exchange(l):
            bcci = P.buf("cci"); bcco = P.buf("cco")
            DMA("sync", "ccst", cc_in[l].ap(), ccs[:], r=[bccs], w=[bcci])
            cin = cc_in[l].ap().opt()
            cout = cc_out[l].ap().opt()
            P.dma("gpsimd", "cc",
                  lambda e, cin=cin, cout=cout: e.collective_compute(
                      "AllGather", ALU.bypass, replica_groups=[list(range(NCORE))], ins=[cin], outs=[cout]),
                  reads=[bcci], writes=[bcco], inc=1)
            xbuf = [(ccr, bccr), (ccs, bccs)]
            for r in range(2):
                DMA("sync", "ccld%d" % r, xbuf[r][0][:], cc_out[l].ap()[r * 128:(r + 1) * 128, :], r=[bcco],
                    w=[xbuf[r][1]])

            def combine():
                for h in range(4):
                    V("memset", w=[bS32[h]], ap=S32[:, h, :], constant=0.0)
                for r in range(NCORE):
                    cb_, bcb_ = xbuf[r % 2]
                    am = cstt[:, 1217 + r:1218 + r]
                    nam = cstt[:, 1225 + r:1226 + r]
                    V("tensor_scalar", r=[bcb_, bcst], w=[bagt], out=agt[:, 512:516], in0=cb_[:, 512:516],
                      scalar1=am, scalar2=nam, op0=ALU.mult, op1=ALU.add)
                    V("tensor_scalar", r=[bcb_, bcst], w=[bagt], out=agt[:, 0:512], in0=cb_[:, 0:512], scalar1=am,
                      scalar2=None, op0=ALU.mult)
                    if r + 2 < NCORE:
                        DMA("sync", "ccld%d" % (r % 2), cb_[:], cc_out[l].ap()[(r + 2) * 128:(r + 3) * 128, :],
                            r=[bcco], w=[bcb_])
                    for h in range(4):
                        V("scalar_tensor_tensor", r=[bS32[h], bagt], w=[bS32[h]], out=S32[:, h, :], in0=S32[:, h, :],
                          scalar=agt[:, 512 + h:513 + h], in1=agt[:, h * 128:(h + 1) * 128], op0=ALU.mult, op1=ALU.add)
            return combine

        def mixer_tile(l, ti, c0, nC, ids, have_norm=False, pre_state=None):
            T = nC * 64
            pb_ = l * NPP
            if not have_norm:
                emit_norm(c0, T, pb_ + 0)
            dd = (l == 0 and ti == 1)
            if dd:
                dump("xn0", xn[:, 0, 2:2 + T], [bxn[0]], T)
                dump("rstd", rstd[:, 0:T], [brstd], T)
            P.label = "Hproj"
            for j, (dstt_, bdst_, fn_, sc_) in enumerate(((sq, bsq, AF.Silu, 1.0), (thn, bthn, AF.Tanh, -0.5),
                                                          (szo, bszo, AF.Silu, 1.0))):
                s = use_slab(ids["QFO"][j])
                for h in range(4):
                    bank, bb = proj_chunk(s, slice(h * 128, (h + 1) * 128), T)
                    A("activation", r=[bb], w=[bdst_[h]], out=dstt_[:, h, 0:T], in_=bank[:, 0:T], func=fn_, scale=sc_)
            sI = use_slab(ids["I"])
            n0 = (l == 1 and c0 == 0)
            W = 16 + T

            def vproj(h):
                P.label = "Hproj"
                proj_v(sI, slice(h * 128, (h + 1) * 128), nC, h)

            def pool_group(g, min_live):
                P.label = "pool"
                s = use_slab(ids["Wp"], min_live=min_live)
                j = 0
                w = 2 << g
                bank, bb = proj_chunk(s, slice(g * 128, (g + 1) * 128), T)
                V("tensor_copy", r=[bpcar], w=[but[j]], out=ut[:, j, 0:16], in_=pcar[:, g, :])
                A("activation", r=[bb], w=[but[j]], out=ut[:, j, 16:W], in_=bank[:, 0:T], func=AF.Identity)
                src, bsrc = ut[:, j, :], but[j]
                tmps = [(psA, bpsA), (psB, bpsB)]
                for lev in range(g + 1):
                    sh = 1 << lev
                    dstt, bdst = tmps[lev % 2]
                    V("tensor_tensor", r=[bsrc], w=[bdst], out=dstt[:, sh:W], in0=src[:, sh:W], in1=src[:, 0:W - sh],
                      op=ALU.add)
                    src, bsrc = dstt, bdst
                V("scalar_tensor_tensor", r=[bsrc, but[j]], w=[bpooled[g]], out=pooled[:, g, 0:T], in0=src[:, 16:W],
                  scalar=1.0 / w, in1=ut[:, j, 16:W], op0=ALU.mult, op1=ALU.subtract)
                if c0 <= 128 and c0 + T >= 144:
                    o = 128 - c0
                    V("tensor_tensor", r=[bsrc, bcst], w=[brs1], out=rs1[:, 0:16], in0=src[:, 16 + o:32 + o],
                      in1=cstt[:, 1152 + g * 16:1168 + g * 16], op=ALU.mult)
                    V("tensor_tensor", r=[brs1, but[j]], w=[bpooled[g]], out=pooled[:, g, o:o + 16], in0=rs1[:, 0:16],
                      in1=ut[:, j, 16 + o:32 + o], op=ALU.subtract)
                V("tensor_copy", r=[but[j]], w=[bpcar], out=pcar[:, g, :], in_=ut[:, j, T:T + 16])

            def pool_b(g):
                P.label = "pool"
                bank2, bb2 = mmbank()
                PE("matmul", r=[bpwb, bpooled[g]], w=[bb2], out=bank2[:, 0:T], lhsT=pwb[:, l * 4 + g, :],
                   rhs=pooled[:, g, 0:T], start=True, stop=True)
                A("activation", r=[bb2, bpp], w=[bmixed[g]], out=mixed[:, g, 0:T], in_=bank2[:, 0:T],
                  func=AF.Identity, scale=ppt[:, pb_ + 24 + g:pb_ + 25 + g])

            if ti == 0:
                V("memset", w=[bpcar], ap=pcar[:], constant=0.0)
            hgrn_s1(l, 0, nC, True, n0, HS[0])
            hgrn_s1(l, 1, nC, True, n0, HS[1])
            vproj(0)
            vproj(1)
            if pre_state is not None:
                pre_state()
            hgrn_s2(l, 0, nC, True, HS[0], n0)
            hgrn_out_a(l, 0, nC, HS[0])
            hgrn_s2(l, 1, nC, True, HS[1], n0)
            hgrn_out_b(l, 0, nC, HS[0])
            vproj(2)
            hgrn_s1(l, 2, nC, True, n0, HS[0])
            pool_group(0, ids["I"])
            hgrn_out_a(l, 1, nC, HS[1])
            hgrn_s2(l, 2, nC, True, HS[0], n0)
            hgrn_out_b(l, 1, nC, HS[1])
            vproj(3)
            hgrn_s1(l, 3, nC, True, n0, HS[1])
            pool_b(0)
            pool_group(1, ids["Wp"])
            hgrn_out_a(l, 2, nC, HS[0])
            hgrn_s2(l, 3, nC, True, HS[1], n0)
            hgrn_out_b(l, 2, nC, HS[0])
            pool_b(1)
            pool_group(2, ids["Wp"])
            hgrn_out_a(l, 3, nC, HS[1])
            pool_b(2)
            pool_group(3, ids["Wp"])
            hgrn_out_b(l, 3, nC, HS[1])
            pool_b(3)
            P.label = "gates"
            for i in range(4):
                sG = use_slab(ids["GP"][i][0])
                sP = use_slab(ids["GP"][i][1], first=False)
                WP = v3(sP, 8, 256)
                for mm in range(2):
                    m = 2 * i + mm
                    bank, bb = proj_chunk(sG, slice(mm * 128, (mm + 1) * 128), T)
                    A("activation", r=[bb, bhbg], w=[btht[0]], out=tht[:, 0, 0:T], in_=bank[:, 0:T], func=AF.Tanh,
                      scale=0.5, bias=hbg[:, l * 16 + m:l * 16 + m + 1])
                    bya, bbya = mmbank()
                    for g in range(4):
                        PE("matmul", r=[bring[sP], bmixed[g]], w=[bbya], out=bya[:, 0:T],
                           lhsT=WP[:, g, mm * 128:(mm + 1) * 128], rhs=mixed[:, g, 0:T], start=(g == 0), stop=(g == 3))
                    V("scalar_tensor_tensor", r=[btht[0], bbya], w=[bAt], out=At[:, 0:T], in0=tht[:, 0, 0:T],
                      scalar=1.0, in1=bya[:, 0:T], op0=ALU.add, op1=ALU.mult)
                    bank, bb = proj_chunk(sG, slice(256 + mm * 128, 256 + (mm + 1) * 128), T)
                    A("activation", r=[bb, bhbg], w=[btht[1]], out=tht[:, 1, 0:T], in_=bank[:, 0:T], func=AF.Tanh,
                      scale=0.5, bias=hbg[:, l * 16 + 8 + m:l * 16 + 9 + m])
                    byb, bbyb = mmbank()
                    for h in range(4):
                        PE("matmul", r=[bring[sP], bonb[h]], w=[bbyb], out=byb[:, 0:T],
                           lhsT=WP[:, 4 + h, mm * 128:(mm + 1) * 128], rhs=onb[:, h, 0:T], start=(h == 0), stop=(h == 3))
                    V("scalar_tensor_tensor", r=[btht[1], bbyb], w=[bBt], out=Bt[:, 0:T], in0=tht[:, 1, 0:T],
                      scalar=1.0, in1=byb[:, 0:T], op0=ALU.add, op1=ALU.mult)
                    V("tensor_tensor", r=[bAt, bBt], w=[bMb[m]], out=Mb[:, m, 0:T], in0=At[:, 0:T], in1=Bt[:, 0:T],
                      op=ALU.add)
                    if dd and m == 0:
                        dump("mixed0", mixed[:, 0, 0:T], [bmixed[0]], T)
                        dump("tht0", tht[:, 0, 0:T], [btht[0]], T)
                        dump("At0", At[:, 0:T], [bAt], T)
                        dump("tht1", tht[:, 1, 0:T], [btht[1]], T)
                        dump("Bt0", Bt[:, 0:T], [bBt], T)
                        dump("Mb0", Mb[:, 0, 0:T], [bMb[0]], T)
            P.label = "wo"
            for i in range(2):
                sO = use_slab(ids["O"][i])
                WO = v3(sO, 8, 512)
                for mm in range(4):
                    m = 4 * i + mm
                    bank, bb = mmbank()
                    for k in range(KC):
                        PE("matmul", r=[bring[sO], bMb[k]], w=[bb], out=bank[:, 0:T],
                           lhsT=WO[:, k, mm * 128:(mm + 1) * 128], rhs=Mb[:, k, 0:T], start=(k == 0), stop=(k == KC - 1))
                    V("scalar_tensor_tensor", r=[bb, bx[m]], w=[bx[m]], out=x32[:, m, c0:c0 + T], in0=bank[:, 0:T],
                      scalar=0.5, in1=x32[:, m, c0:c0 + T], op0=ALU.mult, op1=ALU.add)

        def ffn_tile(l, ti, c0, nC, ids, nxt=None):
            T = nC * 64
            pb_ = l * NPP
            if ti == 0:
                V("memset", w=[bxcar], ap=xcar[:], constant=0.0)
            emit_norm(c0, T, pb_ + 37)
            V("tensor_copy", r=[bxcar], w=bxn, out=xn[:, :, 0:2], in_=xcar[:])
            V("tensor_copy", r=bxn, w=[bxcar], out=xcar[:], in_=xn[:, :, T:T + 2])
            state["mmset"] = [0, 1, 2, 3, 4, 5, 6]
            for half, (ukey, dkey, k0, nk) in enumerate((("U0", "D0", 0, 12), ("U1", "D1", 12, 10))):
                P.label = "up"
                for si, sid in enumerate(ids[ukey]):
                    s = use_slab(sid)
                    W = v3(s, 8, 512)
                    for pp_ in range(2):
                        jj = k0 + 2 * si + pp_
                        ja = 2 * si + pp_
                        res = []
                        for part in range(2):
                            bank, bb = mmbank()
                            cols = slice(part * 256 + pp_ * 128, part * 256 + (pp_ + 1) * 128)
                            for kc in range(KC):
                                PE("matmul", r=[bring[s], bxn[kc]], w=[bb], out=bank[:, 0:T + 2], lhsT=W[:, kc, cols],
                                   rhs=xn[:, kc, 0:T + 2], start=(kc == 0), stop=(kc == KC - 1))
                            res.append((bank, bb))
                        for part, (dst, bdst) in enumerate(((cv, bcv), (cg, bcg))):
                            bank, bb = res[part]
                            ch = jj + part * NFF
                            cb = pb_ + 45 + ch
                            cw = lambda tap, ch=ch: ppt[:, pb_ + 89 + tap * 44 + ch:pb_ + 90 + tap * 44 + ch]
                            A("activation", r=[bb, bpp], w=[bdst], out=dst[:, 0:T], in_=bank[:, 2:T + 2],
                              func=AF.Identity, scale=cw(2), bias=ppt[:, cb:cb + 1])
                            V("scalar_tensor_tensor", r=[bb, bpp, bdst], w=[bdst], out=dst[:, 0:T],
                              in0=bank[:, 1:T + 1], scalar=cw(1), in1=dst[:, 0:T], op0=ALU.mult, op1=ALU.add)
                            V("scalar_tensor_tensor", r=[bb, bpp, bdst], w=[bdst], out=dst[:, 0:T],
                              in0=bank[:, 0:T], scalar=cw(0), in1=dst[:, 0:T], op0=ALU.mult, op1=ALU.add)
                        A("activation", r=[bcg], w=[bsg], out=sg[:, 0:T], in_=cg[:, 0:T], func=AF.Silu)
                        V("tensor_tensor", r=[bsg, bcv], w=[baT[ja]], out=aT[:, ja, 0:T], in0=sg[:, 0:T],
                          in1=cv[:, 0:T], op=ALU.mult)
                P.label = "down"
                for qq in (0, 2):
                    slabs = [use_slab(ids[dkey][qq]), use_slab(ids[dkey][qq + 1], first=False)]
                    groups = []
                    for qi in range(2):
                        W = v3(slabs[qi], nk, 256)
                        for mm in range(2):
                            bank, bb = mmbank()
                            groups.append((W, slabs[qi], mm, 2 * (qq + qi) + mm, bank, bb))
                    for (W, s, mm, m, bank, bb) in groups:
                        for k in range(nk - 2):
                            PE("matmul", r=[bring[s], baT[k]], w=[bb], out=bank[:, 0:T],
                               lhsT=W[:, k, mm * 128:(mm + 1) * 128], rhs=aT[:, k, 0:T], start=(k == 0), stop=False)
                    for (W, s, mm, m, bank, bb) in groups:
                        for k in range(nk - 2, nk):
                            PE("matmul", r=[bring[s], baT[k]], w=[bb], out=bank[:, 0:T],
                               lhsT=W[:, k, mm * 128:(mm + 1) * 128], rhs=aT[:, k, 0:T], start=False, stop=(k == nk - 1))
                        V("tensor_tensor", r=[bb, bx[m]], w=[bx[m]], out=x32[:, m, c0:c0 + T], in0=bank[:, 0:T],
                          in1=x32[:, m, c0:c0 + T], op=ALU.add)
                if half == 1 and nxt is not None:
                    emit_norm(nxt[0], nxt[1] * 64, pb_ + 0)
            state["mmset"] = [0, 1, 2]
            if l == 0 and c0 < 128:
                for m in range(KC):
                    V("tensor_scalar", r=[bx[m], bcst], w=[bx[m]], out=x32[:, m, c0:128], in0=x32[:, m, c0:128],
                      scalar1=cstt[:, 1216:1217], scalar2=None, op0=ALU.mult)

        def store_tile(c0, nC, normed):
            T = nC * 64
            lo = max(c0, HALO)
            o = lo - c0
            n = c0 + T - lo
            if n <= 0:
                return
            if normed:
                norm_rstd(c0, T, D)
            for kc in range(KC):
                j = kc % 2
                if normed:
                    gcol = 221 + kc
                    V("scalar_tensor_tensor", r=[bx[kc], bpp, brstd], w=[bostg[j]], out=ostg[j][:, 0:T],
                      in0=x32[:, kc, c0:c0 + T], scalar=ppt[:, gcol:gcol + 1], in1=rstd[:, 0:T],
                      op0=ALU.mult, op1=ALU.mult)
                else:
                    V("tensor_copy", r=[bx[kc]], w=[bostg[j]], out=ostg[j][:, 0:T], in_=x32[:, kc, c0:c0 + T])
                out_toks.append(DMA("sync", "st%d" % j, outT[kc * 128:(kc + 1) * 128, lo - HALO:lo - HALO + n],
                                    ostg[j][:, o:o + n], r=[bostg[j]]))

        for kind, l, tiles, info in plan:
            if kind == "p1":
                pass1(l, tiles, info)
                pending_combine = exchange(l)
            else:
                do_ffn = info
                last = (not do_ffn) or stop_i == 2 * l + 1 or l == L - 1
                for ti, (c0, nC, tids) in enumerate(tiles):
                    mixer_tile(l, ti, c0, nC, tids, have_norm=(do_ffn and ti > 0),
                               pre_state=(pending_combine if ti == 0 else None))
                    if do_ffn:
                        nx = (tiles[ti + 1][0], tiles[ti + 1][1]) if ti + 1 < len(tiles) else None
                        ffn_tile(l, ti, c0, nC, tids, nxt=nx)
                    if last:
                        store_tile(c0, nC, normed=(stop_after == "full"))
                if last:
                    break
        P.wait_all("sync", out_toks)
        for name in ("tensor", "vector", "scalar", "gpsimd", "sync"):
            print("engine", name, "ops", len(P.eng[name].ops), "insts", P.eng[name].n)
        build_program.pe_labels = list(P.pe_labels)
        P.emit_all()
    return nc


def _pack_params(norm1_g, b_gate, pool_scale, lb_logits, hgrn_norm_g, norm2_g, conv_w, conv_b, final_g):
    pp = np.zeros((128, L * NPP), np.float32)
    for l in range(L):
        b = l * NPP
        pp[:, b + 0:b + 8] = norm1_g[l].reshape(8, 128).T
        pp[:, b + 8:b + 24] = b_gate[l].reshape(16, 128).T
        pp[:, b + 24:b + 28] = pool_scale[l].reshape(4, 128).T
        pp[:, b + 28:b + 32] = lb_logits[0].reshape(4, 128).T
        pp[:, b + 32:b + 36] = lb_logits[1].reshape(4, 128).T
        pp[:, b + 36] = hgrn_norm_g[l]
        pp[:, b + 37:b + 45] = norm2_g[l].reshape(8, 128).T
        pp[:, b + 45:b + 89] = conv_b[l].reshape(44, 128).T
        for tap in range(3):
            pp[:, b + 89 + tap * 44:b + 89 + (tap + 1) * 44] = conv_w[l, tap].reshape(44, 128).T
        pp[:, b + 221:b + 229] = final_g.reshape(8, 128).T
    return pp


def _consts(j, b):
    c = np.zeros((128, NCST), np.float32)
    rm = np.ones(512, np.float32)
    rm[0::64] = 0.0
    c[:, 0:512] = rm[None, :]
    s = np.arange(64)[:, None]
    t = np.arange(64)[None, :]
    cm = (s <= t).astype(np.float32)
    c[0:64, 512:1024] = np.tile(cm, (1, 8))
    c[:, 1024:1152] = np.eye(128, dtype=np.float32)
    for g in range(4):
        w = 2 << g
        if j == 0:
            cnt = np.minimum(np.arange(1, 17), w).astype(np.float32)
        else:
            cnt = np.full(16, w, np.float32)
        c[:, 1152 + g * 16:1168 + g * 16] = (1.0 / cnt)[None, :]
    c[:, 1216] = 0.0 if j == 0 else 1.0
    me = b * 4 + j
    for r in range(NCORE):
        a = 1.0 if (r // 4 == b and r < me) else 0.0
        c[:, 1217 + r] = a
        c[:, 1225 + r] = 1.0 - a
    return c


_PROG_CACHE = {}


def _run(inputs, stop_after="full", debug=False):
    x = np.asarray(inputs["x"], np.float32)
    f = lambda k: np.ascontiguousarray(np.asarray(inputs[k], np.float32))
    pp = _pack_params(f("norm1_g"), f("b_gate"), f("pool_scale"), f("lb_logits"), f("hgrn_norm_g"),
                      f("norm2_g"), f("conv_w"), f("conv_b"), f("final_g"))
    shared = {k: f(k) for k in ("w_in", "pool_w", "w_pa", "w_pb", "w_o", "w_up", "w_down")}
    in_maps = []
    for c in range(NCORE):
        b, j = divmod(c, 4)
        s = j * OWN
        xe = np.zeros((EXT, D), np.float32)
        lo = s - HALO
        if lo < 0:
            xe[-lo:] = x[b, 0:s + OWN]
        else:
            xe[:] = x[b, lo:s + OWN]
        m = {"xT": np.ascontiguousarray(xe.T), "pp": pp, "cst": _consts(j, b)}
        m.update(shared)
        in_maps.append(m)
    key = (stop_after, debug)
    if key not in _PROG_CACHE:
        _PROG_CACHE[key] = build_program(stop_after, debug)
    nc = _PROG_CACHE[key]
    res = run_bass_kernel_spmd(nc, in_maps, core_ids=list(range(NCORE)))
    out = np.empty((NB, SEQ, D), np.float32)
    for c in range(NCORE):
        b, j = divmod(c, 4)
        out[b, j * OWN:(j + 1) * OWN, :] = res.results[c]["outT"].T
    if debug:
        return out, [res.results[c]["dbg"] for c in range(NCORE)], list(build_program.dbg_names)
    return out


def kernel(**inputs):
    return _run(inputs, "full")
```
